# Optimizing a Trainium2 kernel written in Bass

```python
import jax
import jax.numpy as jnp
from jax import lax
import numpy as np

D_MODEL = 2048
BATCH = 1
SEQ = 8192
DEPTH = 4

D_MIX = D_MODEL
HEAD_DIM = 128
HG_HEADS = D_MIX // (4 * HEAD_DIM)
HG_DK = 128
HG_DV = HEAD_DIM
HG_WIDTH = HG_HEADS * HG_DV
HG_CHUNK = 64
HG_MIN_FORGET = 1e-20
NSA_HEADS = D_MIX // (2 * HEAD_DIM)
NSA_KV_HEADS = 2
NSA_GROUP = NSA_HEADS // NSA_KV_HEADS
NSA_WIDTH = NSA_HEADS * HEAD_DIM
NSA_KV_WIDTH = NSA_KV_HEADS * HEAD_DIM
CMP_LEN = 32
CMP_STRIDE = 16
SEL_BLOCK = 64
N_SELECT = 16
WINDOW = 512
WIN_BLOCK = 128
SEL_Q_BLOCK = 64
ML_HEADS = D_MIX // (4 * HEAD_DIM)
ML_DK = HEAD_DIM
ML_DV = HEAD_DIM
ML_WIDTH = ML_HEADS * ML_DV
ML_CHUNK = 64
ML_CONV = 4
MEM_TOKENS = 256
XA_HEADS = 4
XA_HEAD_DIM = D_MODEL // XA_HEADS
D_FF = 4 * D_MODEL
ROPE_THETA = 10000.0
EPS = 1e-6
NEG_INF = -1e30
FORCE_BONUS = 1e4
IN_SIZES = ((HG_HEADS * HG_DK,) * 2 + (HG_WIDTH,) * 2
            + (NSA_WIDTH,) + (NSA_KV_WIDTH,) * 6 + (3 * NSA_HEADS,)
            + (ML_HEADS * ML_DK,) * 2 + (ML_WIDTH,) * 2 + (ML_HEADS, ML_HEADS))
D_IN = sum(IN_SIZES)

kernel_name = 'hymba_style_hgrn2_nsa_mlstm_trunk'


def rms_norm(x, gain):
    xf = x.astype(jnp.float32)
    y = xf * lax.rsqrt(jnp.mean(xf * xf, axis=-1, keepdims=True) + EPS)
    return (y * gain.astype(jnp.float32)).astype(x.dtype)


def rope(x, pos):
    half = x.shape[-1] // 2
    inv_freq = ROPE_THETA ** (-jnp.arange(half, dtype=jnp.float32) / half)
    ang = pos[:, None] * inv_freq[None, :]
    cos = jnp.cos(ang)[:, None, :]
    sin = jnp.sin(ang)[:, None, :]
    xf = x.astype(jnp.float32)
    x1, x2 = xf[..., :half], xf[..., half:]
    return jnp.concatenate([x1 * cos - x2 * sin, x2 * cos + x1 * sin], axis=-1).astype(x.dtype)


def masked_softmax(s, mask):
    p = jax.nn.softmax(jnp.where(mask, s, NEG_INF), axis=-1)
    return jnp.where(mask, p, 0.0)


def causal_conv(x, w):
    c = x.shape[-1]
    return lax.conv_general_dilated(x, w[:, None, :].astype(x.dtype), window_strides=(1,),
                                    padding=[(w.shape[0] - 1, 0)],
                                    dimension_numbers=('NWC', 'WIO', 'NWC'),
                                    feature_group_count=c)


def split_cols(z, sizes):
    parts, start = [], 0
    for s in sizes:
        parts.append(z[..., start:start + s])
        start += s
    return parts


def hgrn2_mixer(q, f_pre, i_in, g, lb, norm_gain):
    b_sz, t_len, _ = q.shape
    lbf = lb.astype(jnp.float32)
    zf = f_pre.astype(jnp.float32)
    forget = lbf + (1.0 - lbf) * jax.nn.sigmoid(zf)
    log_f = jnp.log(jnp.maximum(forget, HG_MIN_FORGET))
    key = (1.0 - lbf) * jax.nn.sigmoid(-zf)
    n_chunks = t_len // HG_CHUNK

    def chunked(a, d):
        return a.astype(jnp.float32).reshape(b_sz, n_chunks, HG_CHUNK, HG_HEADS, d).transpose(1, 0, 3, 2, 4)

    xs = (chunked(q, HG_DK), chunked(log_f, HG_DK), chunked(key, HG_DK), chunked(i_in, HG_DV))
    causal = jnp.tril(jnp.ones((HG_CHUNK, HG_CHUNK), dtype=bool))[:, :, None]

    def step(state, inp):
        qc, lfc, kc, vc = inp
        b = jnp.cumsum(lfc, axis=2)
        diff = b[:, :, :, None, :] - b[:, :, None, :, :]
        decay = jnp.where(causal, jnp.exp(jnp.where(causal, diff, 0.0)), 0.0)
        scores = jnp.einsum('bhtd,bhsd,bhtsd->bhts', qc, kc, decay)
        out = (jnp.einsum('bhtd,bhde->bhte', qc * jnp.exp(b), state)
               + jnp.einsum('bhts,bhse->bhte', scores, vc))
        b_end = b[:, :, -1:, :]
        new_state = (jnp.exp(b_end[:, :, 0, :])[..., None] * state
                     + jnp.einsum('bhsd,bhse->bhde', kc * jnp.exp(b_end - b), vc))
        return new_state, out

    s0 = jnp.zeros((b_sz, HG_HEADS, HG_DK, HG_DV), jnp.float32)
    _, o = lax.scan(step, s0, xs)
    o = o.transpose(1, 0, 3, 2, 4).reshape(b_sz, t_len, HG_HEADS, HG_DV)
    o = rms_norm(o, norm_gain.reshape(HG_HEADS, HG_DV)).reshape(b_sz, t_len, HG_WIDTH)
    return (o * jax.nn.silu(g.astype(jnp.float32))).astype(g.dtype)


def nsa_mixer(q, k_c, v_c, k_s, v_s, k_w, v_w, gate_pre, q_gain, k_gains, cmp_pos, cmp_w):
    dt = q.dtype
    b_sz, t_len, _ = q.shape
    hk, grp, d = NSA_KV_HEADS, NSA_GROUP, HEAD_DIM
    scale = d ** -0.5
    pos = jnp.arange(t_len, dtype=jnp.float32)
    qh = rope(rms_norm(q.reshape(b_sz, t_len, NSA_HEADS, d), q_gain), pos)
    qh = qh.reshape(b_sz, t_len, hk, grp, d).transpose(0, 2, 3, 1, 4)

    def kv_heads(a):
        return a.reshape(b_sz, t_len, hk, d)

    n_cmp = t_len // CMP_STRIDE - 1
    cmp_idx = np.arange(n_cmp)[:, None] * CMP_STRIDE + np.arange(CMP_LEN)[None, :]

    def compress(a, pe, w):
        blocks = kv_heads(a)[:, cmp_idx] + pe[None, None, :, None, :]
        return jnp.einsum('bclgd,lde->bcge', blocks, w)

    cmp_end = jnp.arange(n_cmp, dtype=jnp.float32) * CMP_STRIDE + (CMP_LEN - 1)
    kc = rope(rms_norm(compress(k_c, cmp_pos[0], cmp_w[0]), k_gains[0]), cmp_end).transpose(0, 2, 1, 3)
    vc = compress(v_c, cmp_pos[1], cmp_w[1]).transpose(0, 2, 1, 3)
    cmp_visible = cmp_end[None, :] <= pos[:, None]
    s_cmp = jnp.einsum('bgjtd,bgcd->bgjtc', qh, kc, preferred_element_type=jnp.float32) * scale
    p_cmp = masked_softmax(s_cmp, cmp_visible)
    o_cmp = jnp.einsum('bgjtc,bgcd->bgjtd', p_cmp.astype(dt), vc)

    n_sel = t_len // SEL_BLOCK
    k_top = min(N_SELECT, n_sel)
    c_start = np.arange(n_cmp) * CMP_STRIDE
    s_start = np.arange(n_sel) * SEL_BLOCK
    overlap = ((c_start[:, None] < s_start[None, :] + SEL_BLOCK)
               & (c_start[:, None] + CMP_LEN > s_start[None, :])).astype(np.float32)
    importance = jnp.einsum('bgjtc,cs->bgts', p_cmp, jnp.asarray(overlap))
    cur_blk = np.arange(t_len)[:, None] // SEL_BLOCK
    blk = np.arange(n_sel)[None, :]
    eligible = blk <= cur_blk
    forced = (blk == 0) | (blk == cur_blk) | (blk == cur_blk - 1)
    sel_score = jnp.where(eligible, importance + jnp.where(forced, FORCE_BONUS, 0.0), NEG_INF)
    top_val, top_idx = lax.top_k(sel_score, k_top)
    top_ok = top_val > 0.5 * NEG_INF

    k_sel = rope(rms_norm(kv_heads(k_s), k_gains[1]), pos).transpose(0, 2, 1, 3).reshape(b_sz, hk, n_sel, SEL_BLOCK, d)
    v_sel = kv_heads(v_s).transpose(0, 2, 1, 3).reshape(b_sz, hk, n_sel, SEL_BLOCK, d)
    n_qb = t_len // SEL_Q_BLOCK
    q_blocks = qh.reshape(b_sz, hk, grp, n_qb, SEL_Q_BLOCK, d).transpose(3, 0, 1, 2, 4, 5)
    idx_blocks = top_idx.reshape(b_sz, hk, n_qb, SEL_Q_BLOCK, k_top).transpose(2, 0, 1, 3, 4)
    ok_blocks = top_ok.reshape(b_sz, hk, n_qb, SEL_Q_BLOCK, k_top).transpose(2, 0, 1, 3, 4)
    t_blocks = jnp.arange(t_len).reshape(n_qb, SEL_Q_BLOCK)
    bi = jnp.arange(b_sz)[:, None, None, None]
    gi = jnp.arange(hk)[None, :, None, None]
    offs = jnp.arange(SEL_BLOCK)

    def select_block(args):
        qb, ib, okb, tb = args
        kg = k_sel[bi, gi, ib].reshape(b_sz, hk, SEL_Q_BLOCK, k_top * SEL_BLOCK, d)
        vg = v_sel[bi, gi, ib].reshape(b_sz, hk, SEL_Q_BLOCK, k_top * SEL_BLOCK, d)
        kpos = ib[..., None] * SEL_BLOCK + offs
        mask = (okb[..., None] & (kpos <= tb[None, None, :, None, None])).reshape(
            b_sz, hk, SEL_Q_BLOCK, k_top * SEL_BLOCK)
        s = jnp.einsum('bgjqd,bgqkd->bgjqk', qb, kg, preferred_element_type=jnp.float32) * scale
        p = masked_softmax(s, mask[:, :, None])
        return jnp.einsum('bgjqk,bgqkd->bgjqd', p.astype(dt), vg)

    o_sel = lax.map(select_block, (q_blocks, idx_blocks, ok_blocks, t_blocks))
    o_sel = o_sel.transpose(1, 2, 3, 0, 4, 5).reshape(b_sz, hk, grp, t_len, d)

    n_wb = t_len // WIN_BLOCK
    n_back = WINDOW // WIN_BLOCK
    band = np.arange(n_wb)[:, None] + np.arange(n_back + 1)[None, :]

    def banded(a):
        blocks = a.transpose(0, 2, 1, 3).reshape(b_sz, hk, n_wb, WIN_BLOCK, d)
        padded = jnp.pad(blocks, ((0, 0), (0, 0), (n_back, 0), (0, 0), (0, 0)))
        return padded[:, :, band].reshape(b_sz, hk, n_wb, (n_back + 1) * WIN_BLOCK, d)

    k_band = banded(rope(rms_norm(kv_heads(k_w), k_gains[2]), pos))
    v_band = banded(kv_heads(v_w))
    q_pos = np.arange(t_len).reshape(n_wb, WIN_BLOCK)
    k_pos = (np.arange(n_wb)[:, None] - n_back) * WIN_BLOCK + np.arange((n_back + 1) * WIN_BLOCK)[None, :]
    rel = q_pos[:, :, None] - k_pos[:, None, :]
    win_mask = (rel >= 0) & (rel < WINDOW) & (k_pos[:, None, :] >= 0)
    q_win = qh.reshape(b_sz, hk, grp, n_wb, WIN_BLOCK, d)
    s_win = jnp.einsum('bgjnqd,bgnkd->bgjnqk', q_win, k_band, preferred_element_type=jnp.float32) * scale
    p_win = masked_softmax(s_win, win_mask)
    o_win = jnp.einsum('bgjnqk,bgnkd->bgjnqd', p_win.astype(dt), v_band).reshape(b_sz, hk, grp, t_len, d)

    gates = jax.nn.sigmoid(gate_pre.astype(jnp.float32)).reshape(b_sz, t_len, hk, grp, 3).transpose(0, 2, 3, 1, 4)
    o = gates[..., 0:1] * o_cmp + gates[..., 1:2] * o_sel + gates[..., 2:3] * o_win
    return o.transpose(0, 3, 1, 2, 4).reshape(b_sz, t_len, NSA_WIDTH).astype(dt)


def mlstm_mixer(q, k, v, o_pre, i_pre, f_pre, conv_w, norm_gain):
    b_sz, t_len, _ = q.shape
    qk = jax.nn.silu(causal_conv(jnp.concatenate([q, k], axis=-1), conv_w))
    q, k = qk[..., :ML_HEADS * ML_DK], qk[..., ML_HEADS * ML_DK:]
    n_chunks = t_len // ML_CHUNK

    def chunked(a, d):
        return a.astype(jnp.float32).reshape(b_sz, n_chunks, ML_CHUNK, ML_HEADS, d).transpose(1, 0, 3, 2, 4)

    def chunked_gate(a):
        return a.astype(jnp.float32).reshape(b_sz, n_chunks, ML_CHUNK, ML_HEADS).transpose(1, 0, 3, 2)

    xs = (chunked(q, ML_DK) * ML_DK ** -0.5, chunked(k, ML_DK), chunked(v, ML_DV),
          chunked_gate(jax.nn.log_sigmoid(f_pre.astype(jnp.float32))),
          chunked_gate(i_pre))
    causal = jnp.tril(jnp.ones((ML_CHUNK, ML_CHUNK), dtype=bool))

    def step(carry, inp):
        c_mem, n_mem, m_prev = carry
        qc, kc, vc, lfc, lic = inp
        b = jnp.cumsum(lfc, axis=-1)
        d_log = jnp.where(causal, b[..., :, None] - b[..., None, :] + lic[..., None, :], NEG_INF)
        inter_log = b + m_prev[..., None]
        m_t = jnp.maximum(inter_log, jnp.max(d_log, axis=-1))
        w_intra = jnp.exp(d_log - m_t[..., None])
        w_inter = jnp.exp(inter_log - m_t)
        s = jnp.einsum('bhtd,bhsd->bhts', qc, kc) * w_intra
        num = (w_inter[..., None] * jnp.einsum('bhtd,bhde->bhte', qc, c_mem)
               + jnp.einsum('bhts,bhse->bhte', s, vc))
        qn = w_inter * jnp.einsum('bhtd,bhd->bht', qc, n_mem) + jnp.sum(s, axis=-1)
        h = num / jnp.maximum(jnp.abs(qn), jnp.exp(-m_t))[..., None]
        b_end = b[..., -1]
        w_state = b_end[..., None] - b + lic
        m_new = jnp.maximum(b_end + m_prev, jnp.max(w_state, axis=-1))
        carry_decay = jnp.exp(b_end + m_prev - m_new)
        k_w = kc * jnp.exp(w_state - m_new[..., None])[..., None]
        c_new = carry_decay[..., None, None] * c_mem + jnp.einsum('bhsd,bhse->bhde', k_w, vc)
        n_new = carry_decay[..., None] * n_mem + jnp.sum(k_w, axis=-2)
        return (c_new, n_new, m_new), h

    init = (jnp.zeros((b_sz, ML_HEADS, ML_DK, ML_DV), jnp.float32),
            jnp.zeros((b_sz, ML_HEADS, ML_DK), jnp.float32),
            jnp.zeros((b_sz, ML_HEADS), jnp.float32))
    _, h = lax.scan(step, init, xs)
    h = h.transpose(1, 0, 3, 2, 4).reshape(b_sz, t_len, ML_HEADS, ML_DV)
    h = rms_norm(h, norm_gain.reshape(ML_HEADS, ML_DV)).reshape(b_sz, t_len, ML_WIDTH)
    return (h * jax.nn.sigmoid(o_pre.astype(jnp.float32))).astype(o_pre.dtype)


def memory_cross_attention(h, mem_n, wq, wk, wv, wo, q_gain, k_gain):
    b_sz, t_len, _ = h.shape
    m_len = mem_n.shape[1]
    q = rms_norm((h @ wq).reshape(b_sz, t_len, XA_HEADS, XA_HEAD_DIM), q_gain)
    k = rms_norm((mem_n @ wk).reshape(b_sz, m_len, XA_HEADS, XA_HEAD_DIM), k_gain)
    v = (mem_n @ wv).reshape(b_sz, m_len, XA_HEADS, XA_HEAD_DIM)
    s = jnp.einsum('bthd,bmhd->bhtm', q, k, preferred_element_type=jnp.float32) * XA_HEAD_DIM ** -0.5
    p = jax.nn.softmax(s, axis=-1)
    o = jnp.einsum('bhtm,bmhd->bthd', p.astype(v.dtype), v).reshape(b_sz, t_len, D_MODEL)
    return o @ wo


def setup_inputs(seed: int = 0) -> dict:
    key = jax.random.key(seed)
    ks = jax.random.split(key, 26)
    f32 = jnp.float32

    def normal(k, shape, scale):
        return jax.random.normal(k, shape, f32) * scale

    def gain(k, shape):
        return 1.0 + 0.02 * jax.random.normal(k, shape, f32)

    b_in = normal(ks[3], (DEPTH, D_IN), 0.02)
    b_in = b_in.at[:, D_IN - ML_HEADS:].add(jnp.linspace(3.0, 6.0, ML_HEADS, dtype=f32))
    return {
        'x': normal(ks[0], (BATCH, SEQ, D_MODEL), 1.0),
        'mem': normal(ks[1], (BATCH, MEM_TOKENS, D_MODEL), 1.0),
        'norm_mix': gain(ks[2], (DEPTH, D_MODEL)),
        'w_in': normal(ks[4], (DEPTH, D_MODEL, D_IN), D_MODEL ** -0.5),
        'b_in': b_in,
        'hgrn_lb_logits': normal(ks[5], (DEPTH, HG_HEADS * HG_DK), 0.1),
        'hgrn_norm': gain(ks[6], (DEPTH, HG_WIDTH)),
        'nsa_q_norm': gain(ks[7], (DEPTH, HEAD_DIM)),
        'nsa_k_norm': gain(ks[8], (DEPTH, 3, HEAD_DIM)),
        'nsa_cmp_pos': normal(ks[9], (DEPTH, 2, CMP_LEN, HEAD_DIM), 0.02),
        'nsa_cmp_w': normal(ks[10], (DEPTH, 2, CMP_LEN, HEAD_DIM, HEAD_DIM), (CMP_LEN * HEAD_DIM) ** -0.5),
        'mlstm_conv': normal(ks[11], (DEPTH, ML_CONV, 2 * ML_HEADS * ML_DK), ML_CONV ** -0.5),
        'mlstm_norm': gain(ks[12], (DEPTH, ML_WIDTH)),
        'w_out': normal(ks[13], (DEPTH, D_MIX, D_MODEL), 0.5 * D_MIX ** -0.5),
        'norm_xattn': gain(ks[14], (DEPTH, D_MODEL)),
        'norm_mem': gain(ks[15], (DEPTH, D_MODEL)),
        'xa_wq': normal(ks[16], (DEPTH, D_MODEL, D_MODEL), D_MODEL ** -0.5),
        'xa_wk': normal(ks[17], (DEPTH, D_MODEL, D_MODEL), D_MODEL ** -0.5),
        'xa_wv': normal(ks[18], (DEPTH, D_MODEL, D_MODEL), D_MODEL ** -0.5),
        'xa_wo': normal(ks[19], (DEPTH, D_MODEL, D_MODEL), 0.5 * D_MODEL ** -0.5),
        'xa_q_norm': gain(ks[20], (DEPTH, XA_HEAD_DIM)),
        'xa_k_norm': gain(ks[21], (DEPTH, XA_HEAD_DIM)),
        'norm_mlp': gain(ks[22], (DEPTH, D_MODEL)),
        'mlp_w1': normal(ks[23], (DEPTH, D_MODEL, D_FF), D_MODEL ** -0.5),
        'mlp_w2': normal(ks[24], (DEPTH, D_FF, D_MODEL), 0.5 * D_FF ** -0.5),
    }


def reference(x, mem, norm_mix, w_in, b_in, hgrn_lb_logits, hgrn_norm, nsa_q_norm, nsa_k_norm,
              nsa_cmp_pos, nsa_cmp_w, mlstm_conv, mlstm_norm, w_out, norm_xattn, norm_mem,
              xa_wq, xa_wk, xa_wv, xa_wo, xa_q_norm, xa_k_norm, norm_mlp, mlp_w1, mlp_w2):
    probs = jax.nn.softmax(hgrn_lb_logits.astype(jnp.float32), axis=0)
    lower_bounds = jnp.cumsum(probs, axis=0) - probs[0:1]
    for layer in range(DEPTH):
        h = rms_norm(x, norm_mix[layer])
        z = h @ w_in[layer] + b_in[layer]
        (hg_q, hg_f, hg_i, hg_g,
         ns_q, ns_kc, ns_vc, ns_ks, ns_vs, ns_kw, ns_vw, ns_gate,
         ml_q, ml_k, ml_v, ml_o, ml_i, ml_f) = split_cols(z, IN_SIZES)
        y_hg = hgrn2_mixer(hg_q, hg_f, hg_i, hg_g, lower_bounds[layer], hgrn_norm[layer])
        y_ns = nsa_mixer(ns_q, ns_kc, ns_vc, ns_ks, ns_vs, ns_kw, ns_vw, ns_gate,
                         nsa_q_norm[layer], nsa_k_norm[layer], nsa_cmp_pos[layer], nsa_cmp_w[layer])
        y_ml = mlstm_mixer(ml_q, ml_k, ml_v, ml_o, ml_i, ml_f, mlstm_conv[layer], mlstm_norm[layer])
        y = jnp.concatenate([y_hg, y_ns, y_ml], axis=-1)
        x = x + y @ w_out[layer]
        x = x + memory_cross_attention(rms_norm(x, norm_xattn[layer]), rms_norm(mem, norm_mem[layer]),
                                       xa_wq[layer], xa_wk[layer], xa_wv[layer], xa_wo[layer],
                                       xa_q_norm[layer], xa_k_norm[layer])
        hm = rms_norm(x, norm_mlp[layer])
        x = x + jnp.square(jax.nn.relu(hm @ mlp_w1[layer])) @ mlp_w2[layer]
    return x
```

```python
from contextlib import ExitStack
import math
import numpy as np
import ml_dtypes
import concourse.bass as bass
import concourse.mybir as mybir
from concourse.bass_utils import run_bass_kernel_spmd

F32 = mybir.dt.float32
BF16 = mybir.dt.bfloat16
AF = mybir.ActivationFunctionType
ALU = mybir.AluOpType
AX = mybir.AxisListType
NPBF = ml_dtypes.bfloat16

SEM_CAP = 30000
DMA_SLOTS = 12

D = 2048
T = 8192
DEPTH = 4
NCORE = 8
TPC = T // NCORE
TB = 512
D_IN = 6688
DFF = 8192
EPS = 1e-6
MEM = 256
RB, HSZ, VB, NB, GSZ = 1024, 514, 3080, 4104, 1292
ZSPLIT = 3080
_p = [np.arange(1536, 2048), np.arange(6168, 6680)]
for _hh in range(4):
    _p += [np.arange(0 + _hh * 128, 128 + _hh * 128), np.arange(512 + _hh * 128, 640 + _hh * 128),
           np.arange(4632 + _hh * 128, 4760 + _hh * 128), np.arange(5144 + _hh * 128, 5272 + _hh * 128),
           np.array([6680 + _hh]), np.array([6684 + _hh])]
for _pid in range(8):
    _hh, _half = _pid // 2, _pid % 2
    _p += [np.arange(1024 + _hh * 128 + _half * 64, 1024 + _hh * 128 + _half * 64 + 64),
           np.arange(5656 + _hh * 128 + _half * 64, 5656 + _hh * 128 + _half * 64 + 64)]
for _g in range(2):
    _p += [np.arange(o + _g * 128, o + _g * 128 + 128) for o in (3072, 3328, 3584, 3840, 4096, 4352)]
    _p += [np.arange(2048 + _g * 512, 2048 + _g * 512 + 512), np.arange(4608 + _g * 12, 4608 + _g * 12 + 12)]
PERM = np.concatenate(_p)
assert PERM.shape[0] == 6688 and np.unique(PERM).shape[0] == 6688


class Dyn:
    def __init__(self, fn):
        self.fn = fn

    def __getitem__(self, key):
        return Dyn(lambda pid: self.fn(pid)[key])

    def rearrange(self, pat, **kw):
        return Dyn(lambda pid: self.fn(pid).rearrange(pat, **kw))

    def broadcast_to(self, shp):
        return Dyn(lambda pid: self.fn(pid).broadcast_to(shp))


PID_CACHE = {}


class PV:
    def __init__(self, e):
        self.e = e
        self.c = {}

    def get(self, name):
        if name not in self.c:
            if name == "pid":
                self.c[name] = self.e.partition_id()
            else:
                pid = self.get("pid")
                self.c[name] = {"hh": lambda: pid // 2, "g": lambda: pid // 4, "r": lambda: pid % 4}[name]()
        return self.c[name]
PHASE_COUNTER = [0]
DYN_RR = [0]
DYN_QUEUES = ("sync", "pool", "act")


def next_dyn_q():
    q = DYN_QUEUES[DYN_RR[0] % len(DYN_QUEUES)]
    DYN_RR[0] += 1
    return q


class Prog:
    def __init__(self, nc=None):
        self.nc = nc if nc is not None else bass.Bass("TRN2", target_bir_lowering=False)
        self.es = ExitStack()
        self.streams = {e: [] for e in ("pe", "act", "dve", "pool", "sync")}
        self.cnt = {e: 0 for e in self.streams}
        self.sems = {}
        self.dslot_uses = {}
        self.dslot_last = {}
        self.dnext = {q: 0 for q in ("sync", "pool", "act", "dve", "pe")}
        self.lastw = {}
        self.readers = {}
        self.waited = {e: {} for e in self.streams}
        self.nalloc = 0
        self.rr = 0
        PHASE_COUNTER[0] += 1
        self.pid_ = PHASE_COUNTER[0]

    def sb(self, shape, dtype, name=None):
        self.nalloc += 1
        name = f"p{self.pid_}sb{self.nalloc}_{name or ''}"
        return self.es.enter_context(self.nc.sbuf_tensor(name, list(shape), dtype))

    def ps(self, shape, dtype=F32, name=None):
        self.nalloc += 1
        name = f"p{self.pid_}ps{self.nalloc}_{name or ''}"
        return self.es.enter_context(self.nc.psum_tensor(name, list(shape), dtype))

    def dram(self, name, shape, dtype, kind):
        return self.nc.dram_tensor(name, list(shape), dtype, kind=kind).ap()

    def _sem(self, key):
        if key not in self.sems:
            self.sems[key] = self.nc.alloc_semaphore(name=f"p{self.pid_}s_{key[0]}_{key[1]}")
        return self.sems[key]

    def _deps(self, reads, writes):
        evs = []
        for k in reads:
            if k in self.lastw:
                evs.append(self.lastw[k])
        for k in writes:
            if k in self.lastw:
                evs.append(self.lastw[k])
            evs.extend(self.readers.get(k, ()))
        return evs

    def _commit(self, ev, reads, writes):
        for k in reads:
            self.readers.setdefault(k, []).append(ev)
        for k in writes:
            self.lastw[k] = ev
            self.readers[k] = []

    def _waits_for(self, eng, evs):
        need = {}
        for (sk, v) in evs:
            if sk[0] == "pe" and eng == "pe":
                continue
            if self.waited[eng].get(sk, 0) >= v:
                continue
            if need.get(sk, 0) < v:
                need[sk] = v
        for sk, v in need.items():
            self.waited[eng][sk] = v
        return list(need.items())

    def op(self, eng, fn, reads=(), writes=()):
        reads = list(reads); writes = list(writes)
        evs = self._deps(reads, writes)
        waits = self._waits_for(eng, evs)
        n = self.cnt[eng]
        self.cnt[eng] = n + 1
        sk = (eng, n // SEM_CAP)
        ev = (sk, (n % SEM_CAP) + 1)
        self._sem(sk)
        self.streams[eng].append(("op", fn, waits, sk))
        self._commit(ev, reads, writes)
        return ev

    def dma(self, q, out, in_, reads=(), writes=(), **kw):
        reads = list(reads) + list(getattr(self, "default_reads", ())); writes = list(writes)
        slot = self.dnext[q] % DMA_SLOTS
        self.dnext[q] += 1
        dk = ("dma_" + q, slot)
        evs = self._deps(reads, writes)
        if dk in self.dslot_last:
            evs.append(self.dslot_last[dk])
        waits = self._waits_for(q, evs)
        uses = self.dslot_uses.get(dk, 0) + 1
        self.dslot_uses[dk] = uses
        self._sem(dk)
        ev = (dk, 16 * uses)
        self.dslot_last[dk] = ev
        self.streams[q].append(("dma", (out, in_, kw), waits, dk))
        self._commit(ev, reads, writes)
        return ev

    def custom(self, eng, fn, inc, reads=(), writes=()):
        reads = list(reads); writes = list(writes)
        evs = self._deps(reads, writes)
        waits = self._waits_for(eng, evs)
        self.ncustom = getattr(self, "ncustom", 0) + 1
        dk = ("cust", self.ncustom)
        self._sem(dk)
        ev = (dk, inc)
        self.streams[eng].append(("custom", (fn, inc), waits, dk))
        self._commit(ev, reads, writes)
        return ev

    def cc(self, kind, src, dst, reads=(), writes=()):
        return self.custom("pool", lambda e: e.collective_compute(kind, ALU.bypass, replica_groups=[list(range(NCORE))],
                                                                   ins=[src], outs=[dst]), 1, reads, writes)

    def finish(self, final_events=()):
        nc = self.nc
        fin_waits = self._waits_for("sync", list(final_events))
        self.streams["sync"].append(("waitonly", None, fin_waits, None))
        handles = {"pe": "tensor", "act": "scalar", "dve": "vector", "pool": "gpsimd", "sync": "sync"}
        with nc.Block() as block:
            for eng, hname in handles.items():
                stream = self.streams[eng]
                if not stream:
                    continue

                def body(e, stream=stream, eng=eng):
                    def rs(x):
                        if isinstance(x, Dyn):
                            if eng not in PID_CACHE:
                                PID_CACHE[eng] = PV(e)
                            return x.fn(PID_CACHE[eng])
                        return x
                    for kind, payload, waits, sk in stream:
                        for wsk, v in waits:
                            e.wait_ge(self.sems[wsk], v)
                        if kind == "op":
                            payload(e).then_inc(self.sems[sk], 1)
                        elif kind == "dma":
                            out, in_, kw = payload
                            e.dma_start(out=rs(out), in_=rs(in_), **kw).then_inc(self.sems[sk], 16)
                        elif kind == "custom":
                            fn, inc = payload
                            fn(e).then_inc(self.sems[sk], inc)

                getattr(block, hname)(body)
        nc.clear_and_free_semaphores(list(self.sems.values()))
        nc.all_engine_barrier()
        self.es.close()
        return nc


def build_token_phase(do_post, do_pre, cx=None, layer=0):
    P = Prog(cx.nc if cx is not None else None)
    nc = P.nc
    NB = TPC // TB
    din = {}

    def inp(name, shape, dt=F32):
        din[name] = P.dram(name, shape, dt, "ExternalInput")
        return din[name]

    if cx is not None:
        xT_d = cx.ext["xT"] if (not do_post or layer == 0) else cx.xloc
        if do_post:
            memT_d = cx.ext["memT"]
            wout_d = cx.w["w_out"][layer]; wq_d = cx.w["xa_wq"][layer]; wk_d = cx.w["xa_wk"][layer]
            wv_d = cx.w["xa_wv"][layer]; wo_d = cx.w["xa_wo"][layer]
            w1_d = cx.w["mlp_w1"][layer]; w2_d = cx.w["mlp_w2"][layer]
            vpost_d = cx.ext["vpost_all"][layer]
            xout_d = cx.ext["xT_out"] if not do_pre else cx.xloc
            rec4 = cx.rect.rearrange("(c w e) t -> c w e t", w=2, e=64)
            nsf = cx.nsat
            P.dma("sync", cx.rect[:, :], Dyn(lambda pv: cx.recfull[layer].rearrange("r (j t) -> r j t", j=NCORE)[:, bass.ds(pv.get("pid"), 1), :].rearrange("r j t -> r (j t)")), writes=["stg0"])
            P.dma("sync", cx.nsat[:, :], Dyn(lambda pv: cx.nsafull[layer].rearrange("r (j t) -> r j t", j=NCORE)[:, bass.ds(pv.get("pid"), 1), :].rearrange("r j t -> r (j t)")), writes=["stg1"])
            P.default_reads = ["stg0", "stg1"]
        if do_pre:
            win_d = cx.w["w_in"][layer + 1 if do_post else 0]
            vpre_d = cx.ext["vpre_all"][layer + 1 if do_post else 0]
            z_d = cx.zloc
    else:
        xT_d = inp("xT", [D, TPC])
    if do_post and cx is None:
        ohg_d = inp("ohgT", [512, TPC]); oml_d = inp("omlT", [512, TPC]); yns_d = inp("ynsT", [1024, TPC])
        g_d = inp("gT", [512, TPC]); op_d = inp("opT", [512, TPC])
        memT_d = inp("memT", [D, MEM])
        wout_d = inp("w_out", [D, D], BF16); wq_d = inp("wq", [D, D], BF16); wk_d = inp("wk", [D, D], BF16)
        wv_d = inp("wv", [D, D], BF16); wo_d = inp("wo", [D, D], BF16)
        w1_d = inp("w1", [D, DFF], BF16); w2_d = inp("w2", [DFF, D], BF16)
        vpost_d = inp("vpost", [128, 16 * 3 + 4 * 4])
        xout_d = P.dram("xT_out", [D, TPC], F32, "ExternalOutput")
    if do_pre and cx is None:
        win_d = inp("w_in", [D, D_IN], BF16)
        vpre_d = inp("vpre", [128, 16 + 53])
        z_d = P.dram("zT_out", [D_IN, TPC], F32, "ExternalOutput")

    xT = P.sb([128, 16, TB], F32, "xT")
    hT = P.sb([128, 16, TB], BF16, "hT")
    big = P.sb([128, 16, TB], BF16, "big")
    big2 = P.sb([128, 16, TB], BF16, "big2")
    NWB = 2
    wts = [P.sb([128, 16, 512], BF16, f"wt{i}") for i in range(NWB)]
    o32 = P.sb([128, 4, TB], F32, "o32")
    g32 = P.sb([128, 4, TB], F32, "g32")
    pT = P.sb([128, 2, TB], BF16, "pT")
    rden = P.sb([128, TB], F32, "rden")
    sq = [P.sb([128, TB], F32, f"sq{i}") for i in range(2)]
    rstd = P.sb([128, TB], F32, "rstd")
    stage = [P.sb([128, TB], F32, f"stage{i}") for i in range(3)]
    ones32 = P.sb([128, 128], F32, "ones32")
    onesb = P.sb([128, 128], BF16, "onesb")
    P.op("dve", lambda e: e.memset(ones32[:], 1.0), writes=["ones32"])
    P.op("dve", lambda e: e.memset(onesb[:], 1.0), writes=["onesb"])
    gp = [P.ps([128, TB], F32, f"gp{i}") for i in range(4)]
    st = P.ps([128, TB], F32, "st")
    ap_ = [P.ps([128, TB], F32, f"ap{i}") for i in range(3)]

    wi = [0]

    def next_w():
        i = wi[0] % NWB
        wi[0] += 1
        return wts[i], ("wt", i)

    evi = [0]

    def norm_stats(src_fn, skeys, C, Dn, out_rstd=None, out_key="rstd"):
        out_rstd = rstd if out_rstd is None else out_rstd
        for c in range(C):
            b = c % 2
            P.op("act", lambda e, c=c, b=b: e.activation(out=sq[b][:], in_=src_fn(c), func=AF.Square),
                 reads=[skeys[c]], writes=[("sq", b)])
            P.op("pe", lambda e, c=c, b=b: e.matmul(st[:], lhsT=ones32[:], rhs=sq[b][:], start=(c == 0), stop=(c == C - 1)),
                 reads=[("sq", b), "ones32"], writes=["st"])
        P.op("act", lambda e: e.activation(out=out_rstd[:], in_=st[:], func=AF.Sqrt, scale=1.0 / Dn, bias=epsb[:, 0:1]),
             reads=["st", "epsb"], writes=[out_key])
        P.op("dve", lambda e: e.reciprocal(out_rstd[:], out_rstd[:]), reads=[out_key], writes=[out_key])

    epsb = P.sb([128, 1], F32, "epsb")
    P.op("dve", lambda e: e.memset(epsb[:], EPS), writes=["epsb"])

    def gemm(Wd, K, N, rhs_fn, rkey_fn, evac, ntok=TB):
        KG = K // 2048
        for n0 in range(0, N, 512):
            nw = min(512, N - n0)
            nj = (nw + 127) // 128
            for kg in range(KG):
                wt, wkey = next_w()
                src = Wd[kg * 2048:(kg + 1) * 2048, n0:n0 + nw].rearrange("(c p) n -> p c n", p=128)
                P.dma("sync", wt[:, 0:8, 0:nw], src[:, 0:8, :], writes=[(wkey, 0)])
                P.dma("pool", wt[:, 8:16, 0:nw], src[:, 8:16, :], writes=[(wkey, 1)])
                for j in range(nj):
                    cw = min(128, nw - j * 128)
                    for kc in range(16):
                        first = (kg == 0 and kc == 0)
                        last = (kg == KG - 1 and kc == 15)
                        P.op("pe", lambda e, wt=wt, j=j, cw=cw, kc=kc, kg=kg, first=first, last=last:
                             e.matmul(gp[j][0:cw, 0:ntok], lhsT=wt[:, kc, j * 128:j * 128 + cw], rhs=rhs_fn(kg * 16 + kc),
                                      start=first, stop=last),
                             reads=[(wkey, kc // 8), rkey_fn(kg * 16 + kc)], writes=[("gp", j)])
            for j in range(nj):
                cw = min(128, nw - j * 128)
                evac(n0 // 128 + j, cw, gp[j], ("gp", j))

    def load_cols(dst, dkey, src_d, nrows_chunks, b, q="sync"):
        src = src_d.rearrange("(c p) t -> p c t", p=128)
        P.dma(q, dst[:, 0:nrows_chunks, :], src[:, :, b * TB:(b + 1) * TB], writes=[(dkey, c) for c in range(nrows_chunks)])

    if do_post:
        vpost = P.sb([128, 64], F32, "vpost")
        P.dma("sync", vpost[:], vpost_d[:, :], writes=["vpost"])
        G_XA, G_MEM, G_MLP, G_HG, G_ML, G_XQ, G_XK = 0, 16, 32, 48, 52, 56, 60
        memT = xT[:, :, 0:MEM]
        memn = hT[:, :, 0:MEM]
        kT32 = xT[:, :, 0:MEM]
        kT = P.sb([128, 16, MEM], BF16, "kT")
        vtok = P.sb([128, 2, D], BF16, "vtok")
        P.dma("sync", memT, memT_d.rearrange("(c p) m -> p c m", p=128), writes=[("xT", c) for c in range(16)])
        for c in range(16):
            b = c % 2
            P.op("act", lambda e, c=c, b=b: e.activation(out=sq[b][:, 0:MEM], in_=memT[:, c, :], func=AF.Square),
                 reads=[("xT", c)], writes=[("sq", b)])
            P.op("pe", lambda e, c=c, b=b: e.matmul(st[:, 0:MEM], lhsT=ones32[:], rhs=sq[b][:, 0:MEM], start=(c == 0), stop=(c == 15)),
                 reads=[("sq", b), "ones32"], writes=["st"])
        P.op("act", lambda e: e.activation(out=rstd[:, 0:MEM], in_=st[:, 0:MEM], func=AF.Sqrt, scale=1.0 / D, bias=epsb[:, 0:1]),
             reads=["st", "epsb"], writes=["rstd"])
        P.op("dve", lambda e: e.reciprocal(rstd[:, 0:MEM], rstd[:, 0:MEM]), reads=["rstd"], writes=["rstd"])
        for c in range(16):
            P.op("dve", lambda e, c=c: e.scalar_tensor_tensor(out=memn[:, c, :], in0=memT[:, c, :], scalar=vpost[:, G_MEM + c:G_MEM + c + 1],
                                                               in1=rstd[:, 0:MEM], op0=ALU.mult, op1=ALU.mult),
                 reads=[("xT", c), "rstd", "vpost"], writes=[("hT", c)])

        def evac_k(j, cw, ps, pkey):
            P.op("act", lambda e: e.activation(out=kT32[:, j, :], in_=ps[:, 0:MEM], func=AF.Copy), reads=[pkey], writes=[("xT", j)])
        gemm(wk_d, D, D, lambda c: memn[:, c, :], lambda c: ("hT", c), evac_k, ntok=MEM)
        for hh in range(4):
            for c4 in range(4):
                c = hh * 4 + c4; b = c % 2
                P.op("act", lambda e, c=c, b=b: e.activation(out=sq[b][:, 0:MEM], in_=kT32[:, c, :], func=AF.Square),
                     reads=[("xT", c)], writes=[("sq", b)])
                P.op("pe", lambda e, c4=c4, b=b: e.matmul(st[:, 0:MEM], lhsT=ones32[:], rhs=sq[b][:, 0:MEM], start=(c4 == 0), stop=(c4 == 3)),
                     reads=[("sq", b), "ones32"], writes=["st"])
            P.op("act", lambda e: e.activation(out=rstd[:, 0:MEM], in_=st[:, 0:MEM], func=AF.Sqrt, scale=1.0 / 512, bias=epsb[:, 0:1]),
                 reads=["st", "epsb"], writes=["rstd"])
            P.op("dve", lambda e: e.reciprocal(rstd[:, 0:MEM], rstd[:, 0:MEM]), reads=["rstd"], writes=["rstd"])
            for c4 in range(4):
                c = hh * 4 + c4
                P.op("dve", lambda e, c=c, c4=c4: e.scalar_tensor_tensor(out=kT[:, c, :], in0=kT32[:, c, :], scalar=vpost[:, G_XK + c4:G_XK + c4 + 1],
                                                                         in1=rstd[:, 0:MEM], op0=ALU.mult, op1=ALU.mult),
                     reads=[("xT", c), "rstd", "vpost"], writes=[("kT", c)])
        for n0 in range(0, D, 512):
            wt, wkey = next_w()
            src = wv_d[:, n0:n0 + 512].rearrange("(c p) n -> p c n", p=128)
            P.dma("sync", wt[:, 0:8, :], src[:, 0:8, :], writes=[(wkey, 0)])
            P.dma("pool", wt[:, 8:16, :], src[:, 8:16, :], writes=[(wkey, 1)])
            for mt in range(2):
                for kc in range(16):
                    P.op("pe", lambda e, wt=wt, mt=mt, kc=kc: e.matmul(gp[mt][:, :], lhsT=memn[:, kc, mt * 128:(mt + 1) * 128], rhs=wt[:, kc, :],
                                                                      start=(kc == 0), stop=(kc == 15)),
                         reads=[(wkey, kc // 8), ("hT", kc)], writes=[("gp", mt)])
                P.op("act", lambda e, mt=mt, n0=n0: e.activation(out=vtok[:, mt, n0:n0 + 512], in_=gp[mt][:, :], func=AF.Copy),
                     reads=[("gp", mt)], writes=[("vtok", mt, n0)])
        vkeys = [("vtok", mt, n0) for mt in range(2) for n0 in range(0, D, 512)]
    if do_pre:
        vpre = P.sb([128, 69], F32, "vpre")
        P.dma("sync", vpre[:], vpre_d[:, :], writes=["vpre"])

    final_evs = []
    zkeys = []
    for b in range(NB):
        load_cols(xT, "xT", xT_d, 16, b)
        if do_post:
            yT = big2
            for which in range(2):
                goff = G_HG if which == 0 else G_ML
                cbase = 0 if which == 0 else 12
                if cx is None:
                    od = ohg_d if which == 0 else oml_d
                    gd = g_d if which == 0 else op_d
                    load_cols(o32, "o32", od, 4, b)
                    load_cols(g32, "g32", gd, 4, b, q="pool")
                else:
                    for hh in range(4):
                        for half in range(2):
                            r0_ = (2 * hh + half) * 128 + which * 64
                            P.dma("sync", o32[half * 64:(half + 1) * 64, hh, :], cx.rect[r0_:r0_ + 64, b * TB:(b + 1) * TB], writes=[("o32", hh)])
                    grow = 0 if which == 0 else 512
                    P.dma("pool", g32[:, :, :], cx.zloc[grow:grow + 512, b * TB:(b + 1) * TB].rearrange("(c p) t -> p c t", p=128),
                          reads=[("zloc", j_, b) for j_ in range(8)], writes=[("g32", c) for c in range(4)])
                for hh in range(4):
                    norm_stats(lambda c, hh=hh: o32[:, hh, :], [("o32", hh)], 1, 128.0)
                    P.op("act", lambda e, hh=hh, which=which: e.activation(out=g32[:, hh, :], in_=g32[:, hh, :],
                                                                            func=(AF.Silu if which == 0 else AF.Sigmoid)),
                         reads=[("g32", hh)], writes=[("g32", hh)])
                    P.op("dve", lambda e, hh=hh, goff=goff: e.scalar_tensor_tensor(out=o32[:, hh, :], in0=o32[:, hh, :], scalar=vpost[:, goff + hh:goff + hh + 1],
                                                                                    in1=rstd[:], op0=ALU.mult, op1=ALU.mult),
                         reads=[("o32", hh), "rstd", "vpost"], writes=[("o32", hh)])
                    P.op("dve", lambda e, hh=hh, cbase=cbase: e.tensor_tensor(out=yT[:, cbase + hh, :], in0=o32[:, hh, :], in1=g32[:, hh, :], op=ALU.mult),
                         reads=[("o32", hh), ("g32", hh)], writes=[("big2", cbase + hh)])
            for half in range(2):
                if cx is None:
                    load_cols(o32, "o32", yns_d[half * 512:(half + 1) * 512, :], 4, b)
                else:
                    for hh in range(4):
                        for r in range(4):
                            P.dma("sync", o32[:, hh, r * 128:(r + 1) * 128],
                                  nsf[(half * 4 + r) * 128:(half * 4 + r + 1) * 128, b * 512 + hh * 128:b * 512 + hh * 128 + 128],
                                  writes=[("o32", hh)])
                for hh in range(4):
                    P.op("act", lambda e, hh=hh, half=half: e.activation(out=yT[:, 4 + half * 4 + hh, :], in_=o32[:, hh, :], func=AF.Copy),
                         reads=[("o32", hh)], writes=[("big2", 4 + half * 4 + hh)])

            def evac_add(j, cw, ps, pkey):
                P.op("dve", lambda e: e.tensor_tensor(out=xT[:, j, :], in0=ps[:, :], in1=xT[:, j, :], op=ALU.add),
                     reads=[pkey, ("xT", j)], writes=[("xT", j)])
            gemm(wout_d, D, D, lambda c: yT[:, c, :], lambda c: ("big2", c), evac_add)

            def norm_x(goff_tab, tab):
                norm_stats(lambda c: xT[:, c, :], [("xT", c) for c in range(16)], 16, float(D))
                for c in range(16):
                    P.op("dve", lambda e, c=c: e.scalar_tensor_tensor(out=hT[:, c, :], in0=xT[:, c, :], scalar=tab[:, goff_tab + c:goff_tab + c + 1],
                                                                       in1=rstd[:], op0=ALU.mult, op1=ALU.mult),
                         reads=[("xT", c), "rstd", "vpost" if tab is vpost else "vpre"], writes=[("hT", c)])
            norm_x(G_XA, vpost)

            q32 = [o32[:, i, :] for i in range(4)]
            qn = big
            oxT = big2

            def evac_q(j, cw, ps, pkey):
                c4 = j % 4
                P.op("act", lambda e: e.activation(out=q32[c4], in_=ps[:, :], func=AF.Copy), reads=[pkey], writes=[("o32", c4)])
                if c4 == 3:
                    hh = j // 4
                    norm_stats(lambda c: q32[c], [("o32", c) for c in range(4)], 4, 512.0)
                    for c in range(4):
                        P.op("dve", lambda e, c=c, hh=hh: e.scalar_tensor_tensor(out=qn[:, hh * 4 + c, :], in0=q32[c], scalar=vpost[:, G_XQ + c:G_XQ + c + 1],
                                                                                 in1=rstd[:], op0=ALU.mult, op1=ALU.mult),
                             reads=[("o32", c), "rstd", "vpost"], writes=[("big", hh * 4 + c)])
            gemm(wq_d, D, D, lambda c: hT[:, c, :], lambda c: ("hT", c), evac_q)

            sc = 512.0 ** -0.5
            for hh in range(4):
                for mt in range(2):
                    for c4 in range(4):
                        c = hh * 4 + c4
                        P.op("pe", lambda e, mt=mt, c=c, c4=c4: e.matmul(ap_[mt][:, :], lhsT=kT[:, c, mt * 128:(mt + 1) * 128], rhs=qn[:, c, :],
                                                                         start=(c4 == 0), stop=(c4 == 3)),
                             reads=[("kT", c), ("big", c)], writes=[("ap", mt)])
                    P.op("act", lambda e, mt=mt: e.activation(out=pT[:, mt, :], in_=ap_[mt][:, :], func=AF.Exp, scale=sc),
                         reads=[("ap", mt)], writes=[("pT", mt)])
                for mt in range(2):
                    P.op("pe", lambda e, mt=mt: e.matmul(ap_[2][:, :], lhsT=onesb[:], rhs=pT[:, mt, :], start=(mt == 0), stop=(mt == 1)),
                         reads=[("pT", mt), "onesb"], writes=[("ap", 2)])
                P.op("dve", lambda e: e.reciprocal(rden[:], ap_[2][:, :]), reads=[("ap", 2)], writes=["rden"])
                for c4 in range(4):
                    c = hh * 4 + c4
                    for mt in range(2):
                        P.op("pe", lambda e, mt=mt, c=c, c4=c4: e.matmul(gp[c4][:, :], lhsT=vtok[:, mt, c * 128:(c + 1) * 128], rhs=pT[:, mt, :],
                                                                         start=(mt == 0), stop=(mt == 1)),
                             reads=[("pT", mt)] + vkeys, writes=[("gp", c4)])
                    P.op("dve", lambda e, c=c, c4=c4: e.tensor_tensor(out=oxT[:, c, :], in0=gp[c4][:, :], in1=rden[:], op=ALU.mult),
                         reads=[("gp", c4), "rden"], writes=[("big2", c)])
            gemm(wo_d, D, D, lambda c: oxT[:, c, :], lambda c: ("big2", c), evac_add)
            norm_x(G_MLP, vpost)
            hid = big
            for half in range(4):
                def evac_h(j, cw, ps, pkey, half=half):
                    sb_ = stage[j % 3]
                    P.op("act", lambda e: e.activation(out=sb_[:], in_=ps[:, :], func=AF.Relu), reads=[pkey], writes=[("stage", j % 3)])
                    P.op("pool", lambda e: e.tensor_tensor(out=hid[:, j, :], in0=sb_[:], in1=sb_[:], op=ALU.mult),
                         reads=[("stage", j % 3)], writes=[("big", j)])
                gemm(w1_d[:, half * 2048:(half + 1) * 2048], D, 2048, lambda c: hT[:, c, :], lambda c: ("hT", c), evac_h)
                gemm(w2_d[half * 2048:(half + 1) * 2048, :], 2048, D, lambda c: hid[:, c, :], lambda c: ("big", c), evac_add)
            if not do_pre:
                dst = xout_d.rearrange("(c p) t -> p c t", p=128)
                ev = P.dma("sync", dst[:, :, b * TB:(b + 1) * TB], xT[:, :, :], reads=[("xT", c) for c in range(16)])
                final_evs.append(ev)
        if do_pre:
            norm_stats(lambda c: xT[:, c, :], [("xT", c) for c in range(16)], 16, float(D))
            for c in range(16):
                P.op("dve", lambda e, c=c: e.scalar_tensor_tensor(out=hT[:, c, :], in0=xT[:, c, :], scalar=vpre[:, c:c + 1],
                                                                   in1=rstd[:], op0=ALU.mult, op1=ALU.mult),
                     reads=[("xT", c), "rstd", "vpre"], writes=[("hT", c)])

            def evac_z(j, cw, ps, pkey, b=b):
                sb_ = stage[j % 3]
                P.op("act", lambda e: e.activation(out=sb_[0:cw, :], in_=ps[0:cw, :], func=AF.Identity, bias=vpre[0:cw, 16 + j:17 + j]),
                     reads=[pkey, "vpre"], writes=[("stage", j % 3)])
                ev = P.dma("sync", z_d[j * 128:j * 128 + cw, b * TB:(b + 1) * TB], sb_[0:cw, :], reads=[("stage", j % 3)], writes=[("zloc", j, b)])
                final_evs.append(ev)
                zkeys.append(("zloc", j, b))
            gemm(win_d, D, D_IN, lambda c: hT[:, c, :], lambda c: ("hT", c), evac_z)
            if do_post:
                dst = xout_d.rearrange("(c p) t -> p c t", p=128)
                ev = P.dma("sync", dst[:, :, b * TB:(b + 1) * TB], xT[:, :, :], reads=[("xT", c) for c in range(16)])
                final_evs.append(ev)
    if cx is not None and do_pre:
        zl = layer + 1 if do_post else 0
        final_evs.append(P.cc("AllGather", cx.zloc[0:ZSPLIT, :], cx.zfullA[zl][:, :], reads=zkeys, writes=["zfull"]))
        final_evs.append(P.cc("AllGather", cx.zloc[ZSPLIT:D_IN, :], cx.zfullB[zl][:, :], reads=zkeys, writes=["zfull"]))
        final_evs.append(P.cc("AllGather", cx.bar_src[:, :], cx.bar_dst[:, :], reads=["zfull"], writes=["bar"]))
    return P.finish(final_evs)


W_SPECS = [("w_in", D * D_IN), ("w_out", D * D), ("xa_wq", D * D), ("xa_wk", D * D), ("xa_wv", D * D), ("xa_wo", D * D),
           ("mlp_w1", D * DFF), ("mlp_w2", D * DFF)]


def build_cast_phase(cx=None):
    P = Prog(cx.nc if cx is not None else None)
    CH = 8192
    bufs = [P.sb([128, CH], BF16, f"cb{i}") for i in range(4)]
    bi = 0
    evs = []
    for name, n in W_SPECS:
        m = n * DEPTH // NCORE // 128
        if cx is None:
            src = P.dram(name, [128, m], F32, "ExternalInput")
            dst = P.dram(name + "_b", [128, m], BF16, "ExternalOutput")
        else:
            src = cx.ext[name]
            dst = cx.wloc[name]
        keys = []
        for c0 in range(0, m, CH):
            cw = min(CH, m - c0)
            bt = bufs[bi % 4]; key = ("cb", bi % 4); bi += 1
            P.dma("pool", bt[:, 0:cw], src[:, c0:c0 + cw], writes=[key])
            evs.append(P.dma("sync", dst[:, c0:c0 + cw], bt[:, 0:cw], reads=[key], writes=[("wl", name, c0)]))
            keys.append(("wl", name, c0))
        if cx is not None:
            evs.append(P.cc("AllGather", cx.wloc[name][:, :], cx.wfull[name][:, :], reads=keys, writes=[("wf", name)]))
    if cx is not None:
        evs.append(P.cc("AllGather", cx.bar_src[:, :], cx.bar_dst[:, :], reads=[("wf", n_) for n_, _ in W_SPECS], writes=["bar"]))
    return P.finish(evs)


class Ctx:
    def __init__(self):
        self.nc = bass.Bass("TRN2", target_bir_lowering=False)
        nc = self.nc
        self.ext = {}
        self.ext_shapes = {}

        def ext(name, shape, dt=F32, kind="ExternalInput"):
            self.ext[name] = nc.dram_tensor(name, list(shape), dt, kind=kind).ap()
            return self.ext[name]
        self.add_ext = ext

        def internal(name, shape, dt=F32, shared=False):
            if shared:
                return nc.dram_tensor(name, list(shape), dt, kind="Internal", addr_space="Shared").ap()
            return nc.dram_tensor(name, list(shape), dt, kind="Internal").ap()
        ext("xT", [D, TPC]); ext("memT", [D, MEM]); ext("xT_out", [D, TPC], kind="ExternalOutput")
        ext("vpost_all", [DEPTH, 128, 64]); ext("vpre_all", [DEPTH, 128, 69])
        self.wloc = {}; self.wfull = {}; self.w = {}
        for name, n in W_SPECS:
            m = n * DEPTH // NCORE // 128
            ext(name, [128, m])
            self.wloc[name] = internal(name + "_loc", [128, m], BF16)
            self.wfull[name] = internal(name + "_full", [NCORE * 128, m], BF16, shared=True)
            K_, N_ = {"w_in": (D, D_IN), "mlp_w1": (D, DFF), "mlp_w2": (DFF, D)}.get(name, (D, D))
            flat = self.wfull[name].rearrange("r m -> (r m)")
            v = flat.rearrange("(l k n) -> l k n", l=DEPTH, k=K_)
            self.w[name] = [v[l] for l in range(DEPTH)]
        self.xloc = internal("xloc", [D, TPC])
        self.zloc = internal("zloc", [D_IN, TPC])
        self.zfullA = [internal("zfullA", [NCORE * ZSPLIT, TPC], shared=True)] * DEPTH
        self.zfullB = [internal("zfullB", [NCORE * (D_IN - ZSPLIT), TPC], shared=True)] * DEPTH
        self.recloc = internal("recloc", [128, T])
        self.recfull = [internal("recfull", [NCORE * 128, T], shared=True)] * DEPTH
        self.nsaloc = internal("nsaloc", [128, T])
        self.nsafull = [internal("nsafull", [NCORE * 128, T], shared=True)] * DEPTH
        self.recst = internal("recst", [NCORE, HSZ + 128, TPC])
        self.nsst = internal("nsst", [NCORE, 768, TPC])
        self.nsq = internal("nsq", [NCORE, 524, 2, 128])
        self.rect = internal("rect", [NCORE * 128, TPC])
        self.nsat = internal("nsat", [NCORE * 128, TPC])
        self.gst = internal("gst", [1024, TPC])
        self.bar_src = internal("bar_src", [1, 64])
        self.bar_dst = internal("bar_dst", [NCORE, 64], shared=True)


def run_cast(inputs):
    nc = build_cast_phase()
    in_maps = []
    for c in range(NCORE):
        mp = {}
        for name, n in W_SPECS:
            w = inputs[name]
            if name == "w_in":
                w = w[:, :, PERM]
            flat = np.ascontiguousarray(w).reshape(NCORE, 128, -1)
            mp[name] = flat[c]
        in_maps.append(mp)
    res = run_bass_kernel_spmd(nc, in_maps, core_ids=list(range(NCORE)))
    out = {}
    for name, n in W_SPECS:
        full = np.stack([np.asarray(res.results[c][name + "_b"]) for c in range(NCORE)], axis=0)
        shp = list(inputs[name].shape)
        out[name] = full.reshape(shp)
    return out


SBK = 512
NCHK = SBK // 64
NSB = T // SBK
NQT = 16
KWN = 65 * 128
KWP = 416
HD_SCALE = 128.0 ** -0.5


def build_mixer_phase(parts=("hg", "ml", "ns"), nsb=NSB, nqt=NQT, cx=None, layer=0):
    P = Prog(cx.nc)

    def inp(name, shape, dt=F32):
        if name not in cx.ext:
            cx.add_ext(name, shape, dt)
        return cx.ext[name]
    z3a = cx.zfullA[layer].rearrange("(i r) t -> i r t", i=NCORE)
    z3b = cx.zfullB[layer].rearrange("(i r) t -> i r t", i=NCORE)
    if "ns" in parts:
        stg = cx.nsst
        P.dma("sync", cx.nsst[:, :, :], Dyn(lambda pv: z3b[:, NB - ZSPLIT:NB - ZSPLIT + 2 * GSZ, :].rearrange("i (g r) t -> i g r t", g=2)[:, bass.ds(pv.get("g"), 1), 0:768, :].rearrange("i g r t -> i (g r) t")), writes=["stg0"])
        z5 = z3b.rearrange("i r (h q t) -> i r h q t", h=2, q=4)
        P.dma("sync", cx.nsq[:, :, :, :],
              Dyn(lambda pv: z5[:, NB - ZSPLIT:NB - ZSPLIT + 2 * GSZ, :, :, :].rearrange("i (g r) h q t -> i g r h q t", g=2)
                  [:, bass.ds(pv.get("g"), 1), 768:GSZ, :, bass.ds(pv.get("r"), 1), :].rearrange("i g r h q t -> i (g r) (h q) t")), writes=["stg1"])
        P.default_reads = ["stg0", "stg1"]
    else:
        stg = cx.recst
        P.dma("pool", cx.recst[:, 0:HSZ, :], Dyn(lambda pv: z3a[:, RB:RB + 4 * HSZ, :].rearrange("i (h r) t -> i h r t", h=4)[:, bass.ds(pv.get("hh"), 1), :, :].rearrange("i h r t -> i (h r) t")), writes=["stg0"])
        P.dma("pool", cx.recst[:, HSZ:HSZ + 128, :], Dyn(lambda pv: z3b[:, 0:NCORE * 128, :].rearrange("i (p r) t -> i p r t", p=NCORE)[:, bass.ds(pv.get("pid"), 1), :, :].rearrange("i p r t -> i (p r) t")), writes=["stg1"])
        P.default_reads = ["stg0", "stg1"]

    def zr(row0, nrows, t0, n):
        i = t0 // TPC; c0 = t0 % TPC
        return stg[i:i + 1, row0:row0 + nrows, c0:c0 + n].rearrange("a r t -> (a r) t")

    def act(out, in_, func, r, w, **kw):
        return P.op("act", lambda e: e.activation(out=out, in_=in_, func=func, **kw), r, w)

    def tt(eng, out, a, b, op, r, w):
        return P.op(eng, lambda e: e.tensor_tensor(out=out, in0=a, in1=b, op=op), r, w)

    def ts(eng, out, a, s1, s2, op0, op1, r, w):
        if op1 is None:
            return P.op(eng, lambda e: e.tensor_scalar(out=out, in0=a, scalar1=s1, scalar2=None, op0=op0), r, w)
        return P.op(eng, lambda e: e.tensor_scalar(out=out, in0=a, scalar1=s1, scalar2=s2, op0=op0, op1=op1), r, w)

    def stt(out, a, s, b, op0, op1, r, w):
        return P.op("dve", lambda e: e.scalar_tensor_tensor(out=out, in0=a, scalar=s, in1=b, op0=op0, op1=op1), r, w)

    def mm(out, lhsT, rhs, start, stop, r, w):
        return P.op("pe", lambda e: e.matmul(out, lhsT=lhsT, rhs=rhs, start=start, stop=stop), r, w)

    def tr(out, in_, ident_ap, r, w):
        return P.op("pe", lambda e: e.transpose(out, in_, ident_ap), r, w)

    consts_d = inp("ident", [128, 128])
    tri_d = inp("tri64", [64, 64])
    ident = P.sb([128, 128], BF16, "ident")
    ident32 = P.sb([128, 128], F32, "ident32")
    tri = P.sb([64, 64], BF16, "tri")
    P.dma("pool", ident[:], consts_d[:, :], writes=["ident"])
    P.dma("sync", ident32[:], consts_d[:, :], writes=["ident32"])
    P.dma("pool", tri[:], tri_d[:, :], writes=["tri"])
    onesb = P.sb([128, 128], BF16, "onesb")
    ones32 = P.sb([128, 128], F32, "ones32")
    one1 = P.sb([128, 1], F32, "one1")
    epsb = P.sb([128, 1], F32, "epsb")
    P.op("dve", lambda e: e.memset(onesb[:], 1.0), writes=["onesb"])
    P.op("dve", lambda e: e.memset(ones32[:], 1.0), writes=["ones32"])
    P.op("dve", lambda e: e.memset(one1[:], 1.0), writes=["one1"])
    P.op("dve", lambda e: e.memset(epsb[:], EPS), writes=["epsb"])
    rmask = P.sb([128, SBK], F32, "rmask")
    nmask = P.sb([128, SBK], F32, "nmask")
    P.op("dve", lambda e: e.memset(rmask[:], 1.0), writes=["rmask"])
    P.op("dve", lambda e: e.memset(rmask[:, 0::64], 0.0), writes=["rmask"])
    P.op("dve", lambda e: e.memset(nmask[:], 0.0), writes=["nmask"])
    P.op("dve", lambda e: e.memset(nmask[:, 0::64], -1e30), writes=["nmask"])

    pb = [P.ps([128, 512], F32, f"pb{i}") for i in range(8)]
    reckeys = []
    S = [P.sb([128, SBK], F32, f"S{i}") for i in range(8)]
    final = []

    def c3(ap):
        return ap.rearrange("p (c s) -> p c s", s=64)

    if "hg" in parts:
        hlog_d = inp("hg_logit", [128, DEPTH]); hlm_d = inp("hg_lmask", [DEPTH, 128, DEPTH])[layer]
        ohg_d = cx.recloc[0:64, :]
        hvraw = [P.sb([64, SBK], F32, f"hvraw{i}") for i in range(2)]
        hvs = [P.sb([64, NCHK, 64], BF16, f"hvs{i}") for i in range(2)]
        lg = P.sb([128, DEPTH], F32, "lg"); lm = P.sb([128, DEPTH], F32, "lm")
        P.dma("sync", lg[:], hlog_d[:, :], writes=["lg"]); P.dma("sync", lm[:], hlm_d[:, :], writes=["lm"])
        lsm = P.sb([128, 8], F32, "lsm")
        P.op("dve", lambda e: e.tensor_reduce(out=lsm[:, 0:1], in_=lg[:], axis=AX.X, op=ALU.max), ["lg"], ["lsm"])
        ts("dve", lg[:], lg[:], lsm[:, 0:1], None, ALU.subtract, None, ["lg", "lsm"], ["lg"])
        act(lg[:], lg[:], AF.Exp, ["lg"], ["lg"])
        P.op("dve", lambda e: e.tensor_reduce(out=lsm[:, 1:2], in_=lg[:], axis=AX.X, op=ALU.add), ["lg"], ["lsm"])
        P.op("dve", lambda e: e.reciprocal(lsm[:, 1:2], lsm[:, 1:2]), ["lsm"], ["lsm"])
        tt("dve", lg[:], lg[:], lm[:], ALU.mult, ["lg", "lm"], ["lg"])
        P.op("dve", lambda e: e.tensor_reduce(out=lsm[:, 2:3], in_=lg[:], axis=AX.X, op=ALU.add), ["lg"], ["lsm"])
        tt("dve", lsm[:, 2:3], lsm[:, 2:3], lsm[:, 1:2], ALU.mult, ["lsm"], ["lsm"])
        ts("dve", lsm[:, 3:4], lsm[:, 2:3], -1.0, 1.0, ALU.mult, ALU.add, ["lsm"], ["lsm"])
        ts("dve", lsm[:, 4:5], lsm[:, 3:4], -1.0, None, ALU.mult, None, ["lsm"], ["lsm"])
        LB, OML, NOML = lsm[:, 2:3], lsm[:, 3:4], lsm[:, 4:5]

        st32 = P.sb([128, 64], F32, "hst32"); stbf = P.sb([128, 64], BF16, "hstbf")
        P.op("dve", lambda e: e.memset(st32[:], 0.0), writes=["hst32"])
        P.op("dve", lambda e: e.memset(stbf[:], 0.0), writes=["hstbf"])
        hq32 = [P.sb([128, SBK], F32, f"hq32_{i}") for i in range(2)]
        hf32 = [P.sb([128, SBK], F32, f"hf32_{i}") for i in range(2)]
        qtT = [P.sb([128, SBK], BF16, f"qtT{i}") for i in range(2)]
        ktT = [P.sb([128, SBK], BF16, f"ktT{i}") for i in range(2)]
        qbT = [P.sb([128, SBK], BF16, f"qbT{i}") for i in range(2)]
        k2T = [P.sb([128, SBK], BF16, f"k2T{i}") for i in range(2)]
        k2s = [P.sb([64, NCHK, 128], BF16, f"k2s{i}") for i in range(2)]
        STs = [P.sb([64, NCHK, 64], BF16, f"STs{i}") for i in range(2)]
        decs = [P.sb([128, NCHK], F32, f"decs{i}") for i in range(2)]
        osb = [P.sb([64, SBK], F32, f"osb{i}") for i in range(2)]
        A, B_, C_, D_, E_ = S[0], S[1], S[2], S[3], S[4]
        for sb in range(nsb):
            pz = sb % 2; t0 = sb * SBK
            q32 = hq32[pz]; f32 = hf32[pz]
            kq, kf = f"hq32_{pz}", f"hf32_{pz}"
            P.dma("sync", q32[:], zr(0, 128, t0, SBK), writes=[kq])
            P.dma("sync", f32[:], zr(128, 128, t0, SBK), writes=[kf])
            P.dma("sync", hvraw[pz][:], zr(HSZ, 64, t0, SBK), writes=[f"hvraw{pz}"])
            for c in range(NCHK):
                tr(pb[5][0:64, c * 64:(c + 1) * 64], hvraw[pz][:, c * 64:(c + 1) * 64], ident32[0:64, 0:64], [f"hvraw{pz}", "ident32"], ["pb5"])
            act(hvs[pz][:], c3(pb[5][0:64, :]), AF.Copy, ["pb5"], [f"hvs{pz}"])
            act(A[:], f32[:], AF.Sigmoid, [kf], ["A"])
            ts("dve", B_[:], A[:], OML, LB, ALU.mult, ALU.add, ["A", "lsm"], ["B"])
            ts("dve", B_[:], B_[:], 1e-20, None, ALU.max, None, ["B"], ["B"])
            act(B_[:], B_[:], AF.Ln, ["B"], ["B"])
            ts("pool", A[:], A[:], NOML, OML, ALU.mult, ALU.add, ["A", "lsm"], ["A"])
            P.op("dve", lambda e: e.tensor_tensor_scan(out=C_[:], data0=rmask[:], data1=B_[:], initial=0.0, op0=ALU.mult, op1=ALU.add),
                 ["rmask", "B"], ["C"])
            tt("dve", c3(D_[:]), c3(C_[:]), c3(C_[:])[:, :, 31:32].broadcast_to([128, NCHK, 64]), ALU.subtract, ["C"], ["D"])
            act(E_[:], D_[:], AF.Exp, ["D"], ["E"])
            tt("pool", qtT[pz][:], q32[:], E_[:], ALU.mult, [kq, "E"], [f"qtT{pz}"])
            act(E_[:], D_[:], AF.Exp, ["D"], ["E"], scale=-1.0)
            tt("dve", ktT[pz][:], A[:], E_[:], ALU.mult, ["A", "E"], [f"ktT{pz}"])
            act(E_[:], C_[:], AF.Exp, ["C"], ["E"])
            tt("pool", qbT[pz][:], q32[:], E_[:], ALU.mult, [kq, "E"], [f"qbT{pz}"])
            tt("dve", c3(D_[:]), c3(C_[:])[:, :, 63:64].broadcast_to([128, NCHK, 64]), c3(C_[:]), ALU.subtract, ["C"], ["D"])
            act(D_[:], D_[:], AF.Exp, ["D"], ["D"])
            tt("dve", k2T[pz][:], A[:], D_[:], ALU.mult, ["A", "D"], [f"k2T{pz}"])
            act(decs[pz][:], C_[:, 63::64], AF.Exp, ["C"], [f"decs{pz}"])
            for c in range(NCHK):
                mm(pb[0][0:64, c * 64:(c + 1) * 64], ktT[pz][:, c * 64:(c + 1) * 64], qtT[pz][:, c * 64:(c + 1) * 64], True, True,
                   [f"ktT{pz}", f"qtT{pz}"], ["pb0"])
            tt("dve", STs[pz][:], c3(pb[0][0:64, :]), tri[:].unsqueeze(1).broadcast_to([64, NCHK, 64]), ALU.mult, ["pb0", "tri"], [f"STs{pz}"])
            pb1b = pb[1][0:64, :].bitcast(BF16)
            for c in range(NCHK):
                tr(pb1b[:, c * 128:(c + 1) * 128], k2T[pz][:, c * 64:(c + 1) * 64], ident[:], [f"k2T{pz}", "ident"], ["pb1"])
            act(k2s[pz][:], pb1b.rearrange("p (c d) -> p c d", d=128), AF.Copy, ["pb1"], [f"k2s{pz}"])
            for c in range(NCHK):
                ch = sb * NCHK + c
                mm(pb[2][0:64, c * 64:(c + 1) * 64], hvs[pz][:, c, :], STs[pz][:, c, :], True, False, [f"hvs{pz}", f"STs{pz}"], ["pb2"])
                mm(pb[2][0:64, c * 64:(c + 1) * 64], stbf[:], qbT[pz][:, c * 64:(c + 1) * 64], False, True, ["hstbf", f"qbT{pz}"], ["pb2"])
                mm(pb[3][:, 0:64], k2s[pz][:, c, :], hvs[pz][:, c, :], True, True, [f"k2s{pz}", f"hvs{pz}"], ["pb3"])
                stt(st32[:], st32[:], decs[pz][:, c:c + 1], pb[3][:, 0:64], ALU.mult, ALU.add, ["hst32", f"decs{pz}", "pb3"], ["hst32"])
                act(stbf[:], st32[:], AF.Copy, ["hst32"], ["hstbf"])
            act(osb[pz][:], pb[2][0:64, :], AF.Copy, ["pb2"], [f"osb{pz}"])
            final.append(P.dma("sync", ohg_d[:, t0:t0 + SBK], osb[pz][:], reads=[f"osb{pz}"], writes=[("recloc", 0, sb)]))
            reckeys.append(("recloc", 0, sb))

    if "ml" in parts:
        mcw_d = inp("ml_cw", [DEPTH, 128, 8])[layer]
        oml_d = cx.recloc[64:128, :]
        mvraw = [P.sb([64, SBK], F32, f"mvraw{i}") for i in range(2)]
        mvs = [P.sb([64, NCHK, 64], BF16, f"mvs{i}") for i in range(2)]
        cw = P.sb([128, 8], F32, "cw")
        P.dma("sync", cw[:], mcw_d[:, :], writes=["cw"])
        cn32 = P.sb([128, 128], F32, "cn32"); cnbf = P.sb([128, 128], BF16, "cnbf")
        P.op("dve", lambda e: e.memset(cn32[:], 0.0), writes=["cn32"])
        P.op("dve", lambda e: e.memset(cnbf[:], 0.0), writes=["cnbf"])
        mcar = P.sb([128, 1], F32, "mcar")
        P.op("dve", lambda e: e.memset(mcar[:], 0.0), writes=["mcar"])
        xq = [P.sb([128, SBK + 3], F32, f"xq{i}") for i in range(2)]
        xk = [P.sb([128, SBK + 3], F32, f"xk{i}") for i in range(2)]
        gi = [P.sb([128, SBK], F32, f"gi{i}") for i in range(2)]
        gf = [P.sb([128, SBK], F32, f"gf{i}") for i in range(2)]
        q1T = [P.sb([128, SBK], BF16, f"q1T{i}") for i in range(2)]
        q2T = [P.sb([128, SBK], BF16, f"q2T{i}") for i in range(2)]
        k1T = [P.sb([128, SBK], BF16, f"k1T{i}") for i in range(2)]
        kwT = [P.sb([128, SBK], BF16, f"kwT{i}") for i in range(2)]
        kws = [P.sb([64, NCHK, 128], BF16, f"kws{i}") for i in range(2)]
        MSTs = [P.sb([64, NCHK, 64], BF16, f"MSTs{i}") for i in range(2)]
        cdec = [P.sb([128, NCHK], F32, f"cdec{i}") for i in range(2)]
        emt = [P.sb([128, SBK], F32, f"emt{i}") for i in range(2)]
        hsb = [P.sb([64, SBK], F32, f"hsb{i}") for i in range(2)]
        msm = P.sb([128, 4, NCHK], F32, "msm")
        Aq, Ak, Bb, Cc, Dd, Ee, Ff, Gg = S
        for sb in range(nsb):
            pz = sb % 2; t0 = sb * SBK
            kxq, kxk, kgi, kgf = f"xq{pz}", f"xk{pz}", f"gi{pz}", f"gf{pz}"
            for (x_, kx, r0) in ((xq[pz], kxq, 256), (xk[pz], kxk, 384)):
                P.dma("sync", x_[:, 3:SBK + 3], zr(r0, 128, t0, SBK), writes=[kx])
                if t0 == 0:
                    P.op("pool", lambda e, x_=x_: e.memset(x_[:, 0:3], 0.0), [], [kx])
                else:
                    P.dma("sync", x_[:, 0:3], zr(r0, 128, t0 - 3, 3), writes=[kx])
            P.dma("sync", gi[pz][:], zr(512, 1, t0, SBK).broadcast_to([128, SBK]), writes=[kgi])
            P.dma("sync", gf[pz][:], zr(513, 1, t0, SBK).broadcast_to([128, SBK]), writes=[kgf])
            P.dma("sync", mvraw[pz][:], zr(HSZ + 64, 64, t0, SBK), writes=[f"mvraw{pz}"])
            for c in range(NCHK):
                tr(pb[5][0:64, c * 64:(c + 1) * 64], mvraw[pz][:, c * 64:(c + 1) * 64], ident32[0:64, 0:64], [f"mvraw{pz}", "ident32"], ["pb5"])
            act(mvs[pz][:], c3(pb[5][0:64, :]), AF.Copy, ["pb5"], [f"mvs{pz}"])
            for (x_, kx, dst, kd, w0) in ((xq[pz], kxq, Aq, "S0", 0), (xk[pz], kxk, Ak, "S1", 4)):
                ts("dve", dst[:], x_[:, 0:SBK], cw[:, w0:w0 + 1], None, ALU.mult, None, [kx, "cw"], [kd])
                for j in range(1, 4):
                    stt(dst[:], x_[:, j:j + SBK], cw[:, w0 + j:w0 + j + 1], dst[:], ALU.mult, ALU.add, [kx, "cw", kd], [kd])
                act(dst[:], dst[:], AF.Silu, [kd], [kd])
            act(Bb[:], gf[pz][:], AF.Exp, [kgf], ["S2"], scale=-1.0)
            act(Bb[:], Bb[:], AF.Ln, ["S2", "one1"], ["S2"], bias=one1[:, 0:1])
            P.op("dve", lambda e: e.tensor_tensor_scan(out=Cc[:], data0=rmask[:], data1=Bb[:], initial=0.0, op0=ALU.mult, op1=ALU.subtract),
                 ["rmask", "S2"], ["S3"])
            tt("dve", Dd[:], gi[pz][:], Cc[:], ALU.subtract, [kgi, "S3"], ["S4"])
            P.op("dve", lambda e: e.tensor_tensor_scan(out=Ee[:], data0=nmask[:], data1=Dd[:], initial=-1e30, op0=ALU.add, op1=ALU.max),
                 ["nmask", "S4"], ["S5"])
            tt("dve", msm[:, 0, :], Cc[:, 63::64], Ee[:, 63::64], ALU.add, ["S3", "S5"], ["msm"])
            P.op("dve", lambda e: e.tensor_tensor_scan(out=msm[:, 1, :], data0=Cc[:, 63::64], data1=msm[:, 0, :], initial=mcar[:, 0:1],
                                                        op0=ALU.add, op1=ALU.max), ["S3", "msm", "mcar"], ["msm"])
            P.op("dve", lambda e: e.tensor_copy(msm[:, 2, 1:NCHK], msm[:, 1, 0:NCHK - 1]), ["msm"], ["msm"])
            P.op("dve", lambda e: e.tensor_copy(msm[:, 2, 0:1], mcar[:, 0:1]), ["msm", "mcar"], ["msm"])
            P.op("dve", lambda e: e.tensor_copy(mcar[:, 0:1], msm[:, 1, NCHK - 1:NCHK]), ["msm"], ["mcar"])
            tt("dve", msm[:, 3, :], Cc[:, 63::64], msm[:, 1, :], ALU.subtract, ["S3", "msm"], ["msm"])
            tt("dve", msm[:, 0, :], msm[:, 3, :], msm[:, 2, :], ALU.add, ["msm"], ["msm"])
            act(cdec[pz][:], msm[:, 0, :], AF.Exp, ["msm"], [f"cdec{pz}"])
            tt("dve", c3(Ff[:]), c3(Cc[:]), msm[:, 2, :].unsqueeze(2).broadcast_to([128, NCHK, 64]), ALU.add, ["S3", "msm"], ["S6"])
            tt("dve", Ee[:], Cc[:], Ee[:], ALU.add, ["S3", "S5"], ["S5"])
            tt("dve", Ee[:], Ee[:], Ff[:], ALU.max, ["S5", "S6"], ["S5"])
            act(emt[pz][:], Ee[:], AF.Exp, ["S5"], [f"emt{pz}"], scale=-1.0)
            tt("dve", Ff[:], Ff[:], Ee[:], ALU.subtract, ["S6", "S5"], ["S6"])
            act(Ff[:], Ff[:], AF.Exp, ["S6"], ["S6"])
            stt(q2T[pz][:], Aq[:], HD_SCALE, Ff[:], ALU.mult, ALU.mult, ["S0", "S6"], [f"q2T{pz}"])
            tt("dve", Ee[:], Cc[:], Ee[:], ALU.subtract, ["S3", "S5"], ["S5"])
            act(Ee[:], Ee[:], AF.Exp, ["S5"], ["S5"])
            stt(q1T[pz][:], Aq[:], HD_SCALE, Ee[:], ALU.mult, ALU.mult, ["S0", "S5"], [f"q1T{pz}"])
            act(Gg[:], Dd[:], AF.Exp, ["S4"], ["S7"])
            tt("pool", k1T[pz][:], Ak[:], Gg[:], ALU.mult, ["S1", "S7"], [f"k1T{pz}"])
            tt("dve", c3(Dd[:]), c3(Dd[:]), msm[:, 3, :].unsqueeze(2).broadcast_to([128, NCHK, 64]), ALU.add, ["S4", "msm"], ["S4"])
            act(Dd[:], Dd[:], AF.Exp, ["S4"], ["S4"])
            tt("pool", kwT[pz][:], Ak[:], Dd[:], ALU.mult, ["S1", "S4"], [f"kwT{pz}"])
            for c in range(NCHK):
                mm(pb[0][0:64, c * 64:(c + 1) * 64], k1T[pz][:, c * 64:(c + 1) * 64], q1T[pz][:, c * 64:(c + 1) * 64], True, True,
                   [f"k1T{pz}", f"q1T{pz}"], ["pb0"])
            tt("dve", MSTs[pz][:], c3(pb[0][0:64, :]), tri[:].unsqueeze(1).broadcast_to([64, NCHK, 64]), ALU.mult, ["pb0", "tri"], [f"MSTs{pz}"])
            pb1b = pb[1][0:64, :].bitcast(BF16)
            for c in range(NCHK):
                tr(pb1b[:, c * 128:(c + 1) * 128], kwT[pz][:, c * 64:(c + 1) * 64], ident[:], [f"kwT{pz}", "ident"], ["pb1"])
            act(kws[pz][:], pb1b.rearrange("p (c d) -> p c d", d=128), AF.Copy, ["pb1"], [f"kws{pz}"])
            for c in range(NCHK):
                ch = sb * NCHK + c
                cs = slice(c * 64, (c + 1) * 64)
                mm(pb[2][0:64, cs], mvs[pz][:, c, :], MSTs[pz][:, c, :], True, False, [f"mvs{pz}", f"MSTs{pz}"], ["pb2"])
                mm(pb[2][0:64, cs], cnbf[:, 0:64], q2T[pz][:, cs], False, True, ["cnbf", f"q2T{pz}"], ["pb2"])
                mm(pb[4][0:64, cs], onesb[0:64, 0:64], MSTs[pz][:, c, :], True, False, ["onesb", f"MSTs{pz}"], ["pb4"])
                mm(pb[4][0:64, cs], cnbf[:, 64:128], q2T[pz][:, cs], False, True, ["cnbf", f"q2T{pz}"], ["pb4"])
                mm(pb[3][:, 0:64], kws[pz][:, c, :], mvs[pz][:, c, :], True, True, [f"kws{pz}", f"mvs{pz}"], ["pb3"])
                mm(pb[3][:, 64:128], kws[pz][:, c, :], onesb[0:64, 0:64], True, True, [f"kws{pz}", "onesb"], ["pb3"])
                stt(cn32[:], cn32[:], cdec[pz][:, c:c + 1], pb[3][:, 0:128], ALU.mult, ALU.add, ["cn32", f"cdec{pz}", "pb3"], ["cn32"])
                act(cnbf[:], cn32[:], AF.Copy, ["cn32"], ["cnbf"])
            stt(hsb[pz][:], pb[4][0:64, :], -1.0, emt[pz][0:64, :], ALU.mult, ALU.max, ["pb4", f"emt{pz}"], [f"hsb{pz}"])
            tt("dve", hsb[pz][:], pb[4][0:64, :], hsb[pz][:], ALU.max, ["pb4", f"hsb{pz}"], [f"hsb{pz}"])
            P.op("dve", lambda e, pz=pz: e.reciprocal(hsb[pz][:], hsb[pz][:]), [f"hsb{pz}"], [f"hsb{pz}"])
            tt("dve", hsb[pz][:], pb[2][0:64, :], hsb[pz][:], ALU.mult, ["pb2", f"hsb{pz}"], [f"hsb{pz}"])
            final.append(P.dma("sync", oml_d[:, t0:t0 + SBK], hsb[pz][:], reads=[f"hsb{pz}"], writes=[("recloc", 1, sb)]))
            reckeys.append(("recloc", 1, sb))
    if "hg" in parts or "ml" in parts:
        final.append(P.cc("AllGather", cx.recloc[:, :], cx.recfull[layer][:, :], reads=reckeys, writes=["recfull"]))
        final.append(P.cc("AllGather", cx.bar_src[:, :], cx.bar_dst[:, :], reads=["recfull"], writes=["bar"]))
    if "ns" in parts:
        build_nsa(P, inp, final, pb, S, ident, ident32, onesb, ones32, epsb, act, tt, ts, stt, mm, tr, nqt, cx, layer, zr)
    return P.finish(final)


def build_nsa(P, inp, final, pb, S, ident, ident32, onesb, ones32, epsb, act, tt, ts, stt, mm, tr, nqt, cx, layer, zr):
    cosT_d = inp("cosT", [128, T]); sinT_d = inp("sinT", [128, T])
    cw_d = inp("cmp_w", [DEPTH, 2, 128, 32, 128])[layer]; cpe_d = inp("cmp_pe", [DEPTH, 2, 128, 32])[layer]
    kcg_d = inp("kcg", [DEPTH, 128, 128])[layer]; cosC_d = inp("cosC", [128, 4, 64]); sinC_d = inp("sinC", [128, 4, 64])
    cosQ_d = inp("cosQ", [128, NQT, 128]); sinQ_d = inp("sinQ", [128, NQT, 128])
    nvec_d = inp("nvec", [DEPTH, 128, 8])[layer]
    cmask_d = inp("cmask", [128, 4, 128]); cmask2_d = inp("cmask2", [128, 4, 128]); dmask_d = inp("dmask", [128, 4, 128]); wmask_d = inp("wmask8", [128, 8, 128])
    bonus_d = inp("bonus", [128, NQT, 128]); emat_d = inp("emat", [128, T]); ov_d = inp("ov", [128, 4, 128])
    yns_d = cx.nsaloc

    def ld(q, shape, dt, src, name):
        t_ = P.sb(shape, dt, name)
        P.dma(q, t_[:], src, writes=[name])
        return t_

    nvec = ld("sync", [128, 8], F32, nvec_d[:, :], "nvec")
    cmask = ld("pool", [128, 4, 128], BF16, cmask_d[:, :, :], "cmask")
    cmask2 = ld("pool", [128, 4, 128], BF16, cmask2_d[:, :, :], "cmask2")
    dmask = ld("pool", [128, 4, 128], BF16, dmask_d[:, :, :], "dmask")
    wmask = ld("pool", [128, 8, 128], BF16, wmask_d[:, :, :], "wmask")
    bonus = ld("sync", [128, NQT, 128], F32, bonus_d[:, :, :], "bonus")
    emat = ld("pool", [128, T], BF16, emat_d[:, :], "emat")
    ov = ld("sync", [128, 4, 128], F32, ov_d[:, :, :], "ov")
    kcg = ld("sync", [128, 128], F32, kcg_d[:, :], "kcg")
    cosC = ld("sync", [128, 4, 64], F32, cosC_d[:, :, :], "cosC")
    sinC = ld("sync", [128, 4, 64], F32, sinC_d[:, :, :], "sinC")
    vs = P.sb([128, 64, 128], BF16, "vs")
    vw = P.sb([128, 64, 128], BF16, "vw")
    ksT = P.sb([128, T], BF16, "ksT")
    kwT = P.sb([128, T], BF16, "kwT")
    kcT = P.sb([128, 512], F32, "kcT")
    vc = P.sb([128, 4, 128], F32, "vc")

    def grow(base):
        return base

    for (dst, dkey, base) in ((vs, "vs", 384), (vw, "vw", 640)):
        for blk in range(T // 512):
            raw = S[7]
            P.dma("sync", raw[:, :], zr(grow(base), 128, blk * 512, 512), writes=["S7"])
            for k4 in range(4):
                tr(pb[5][:, k4 * 128:(k4 + 1) * 128], raw[:, k4 * 128:(k4 + 1) * 128], ident32[:], ["S7", "ident32"], ["pb5"])
            act(dst[:, blk * 4:(blk + 1) * 4, :], pb[5][:, :].rearrange("p (k e) -> p k e", e=128), AF.Copy, ["pb5"], [dkey])

    aT = P.sb([128, T + 32], BF16, "aT")
    _a32 = aT[:].bitcast(F32)
    S_alt = [_a32[:, i * SBK:(i + 1) * SBK] for i in range(7)]
    SAK = [f"SA{i}" for i in range(7)]

    def rope_cm(load_x, c_src, s_src, n, gcol, dst, dkey, scale, nj=1, dst32=None, dkey32=None, alt=False):
        SS = S_alt if alt else S
        kp_ = "SA" if alt else "S"
        x, xr, cs, sn, t1, t2, rs = SS[0], SS[1], SS[2], SS[3], SS[4], SS[5], SS[6]
        nt = n // nj
        load_x(x, kp_ + "0", False); load_x(xr, kp_ + "1", True)
        P.dma("sync", cs[:, 0:nt], c_src, writes=[kp_ + "2"]); P.dma("sync", sn[:, 0:nt], s_src, writes=[kp_ + "3"])
        act(t1[:, 0:n], x[:, 0:n], AF.Square, [kp_ + "0"], [kp_ + "4"])
        mm(pb[7][:, 0:n], ones32[:], t1[:, 0:n], True, True, [kp_ + "4", "ones32"], ["pb7"])
        act(rs[:, 0:n], pb[7][:, 0:n], AF.Sqrt, ["pb7", "epsb"], [kp_ + "6"], scale=1.0 / 128, bias=epsb[:, 0:1])
        P.op("dve", lambda e: e.reciprocal(rs[:, 0:n], rs[:, 0:n]), [kp_ + "6"], [kp_ + "6"])

        def v3(ap):
            return ap[:, 0:n].rearrange("p (j t) -> p j t", j=nj)

        def b3(ap):
            return ap[:, 0:nt].unsqueeze(1).broadcast_to([128, nj, nt])
        stt(v3(t1), v3(x), nvec[:, gcol:gcol + 1], b3(cs), ALU.mult, ALU.mult, [kp_ + "0", kp_ + "2", "nvec"], [kp_ + "4"])
        stt(v3(t2), v3(xr), nvec[:, gcol + 1:gcol + 2], b3(sn), ALU.mult, ALU.mult, [kp_ + "1", kp_ + "3", "nvec"], [kp_ + "5"])
        tt("pool", t1[:, 0:n], t1[:, 0:n], t2[:, 0:n], ALU.add, [kp_ + "4", kp_ + "5"], [kp_ + "4"])
        if dst32 is None:
            stt(dst, t1[:, 0:n], scale, rs[:, 0:n], ALU.mult, ALU.mult, [kp_ + "4", kp_ + "6"], [dkey])
        else:
            stt(dst32, t1[:, 0:n], scale, rs[:, 0:n], ALU.mult, ALU.mult, [kp_ + "4", kp_ + "6"], [dkey32])
            act(dst, dst32, AF.Copy, [dkey32], [dkey])

    def k_loader(base, t0, n):
        def load(tile, key, rolled):
            if not rolled:
                P.dma("sync", tile[:, 0:n], zr(grow(base), 128, t0, n), writes=[key])
            else:
                P.dma("sync", tile[0:64, 0:n], zr(grow(base + 64), 64, t0, n), writes=[key])
                P.dma("sync", tile[64:128, 0:n], zr(grow(base), 64, t0, n), writes=[key])
        return load

    for (dstT, dkey, base, gcol) in ((ksT, "ksT", 256, 2), (kwT, "kwT", 512, 4)):
        for blk in range(T // 512):
            sl = slice(blk * 512, (blk + 1) * 512)
            rope_cm(k_loader(base, blk * 512, 512), cosT_d[:, sl], sinT_d[:, sl], 512, gcol, dstT[:, sl], dkey, 1.0, alt=(blk % 2 == 1))

    cwb = P.sb([128, 32, 128], BF16, "cwb")
    pe32 = P.sb([128, 32], F32, "pe32")
    perep = P.sb([128, 32, 128], BF16, "perep")
    small = P.sb([128, 8], F32, "nsmall")
    for which in range(2):
        base = 0 if which == 0 else 128
        for i in range(NCORE):
            P.dma("pool", aT[:, i * TPC:(i + 1) * TPC], zr(grow(base), 128, i * TPC, TPC), writes=["aT"] + SAK)
        P.op("dve", lambda e: e.memset(aT[:, T:T + 32], 0.0), [], ["aT"] + SAK)
        P.dma("pool", cwb[:], cw_d[which], writes=["cwb"])
        P.dma("sync", pe32[:], cpe_d[which], writes=["pe32"])
        P.op("dve", lambda e: e.tensor_copy(perep[:], pe32[:].unsqueeze(2).broadcast_to([128, 32, 128])), ["pe32"], ["perep"])
        for ct in range(4):
            for l in range(32):
                b0 = 2048 * ct + l
                mm(pb[6][:, 0:128], aT[:, b0:b0 + 2048:16], cwb[:, l, :], l == 0, False, ["aT", "cwb"] + SAK, ["pb6"])
            for l in range(32):
                mm(pb[6][:, 0:128], perep[:, l, :], cwb[:, l, :], False, l == 31, ["perep", "cwb"], ["pb6"])
            if which == 1:
                act(vc[:, ct, :], pb[6][:, 0:128], AF.Copy, ["pb6"], ["vc"])
                continue
            x = S[0][:, 0:128]; t1 = S[1][:, 0:128]; t2 = S[2][:, 0:128]; xn = S[3][:, 0:128]; kr = S[4][:, 0:128]
            act(x, pb[6][:, 0:128], AF.Copy, ["pb6"], ["S0"])
            P.op("act", lambda e, x=x, t1=t1: e.activation(out=t1, in_=x, func=AF.Square, accum_out=small[:, 0:1]), ["S0"], ["S1", "nsmall"])
            act(small[:, 1:2], small[:, 0:1], AF.Sqrt, ["nsmall", "epsb"], ["nsmall"], scale=1.0 / 128, bias=epsb[:, 0:1])
            P.op("dve", lambda e: e.reciprocal(small[:, 1:2], small[:, 1:2]), ["nsmall"], ["nsmall"])
            stt(xn, x, small[:, 1:2], kcg[:], ALU.mult, ALU.mult, ["S0", "nsmall", "kcg"], ["S3"])
            x1, x2 = xn[:, 0:64], xn[:, 64:128]
            cc_, ss_ = cosC[:, ct, :], sinC[:, ct, :]
            tt("dve", t1[:, 0:64], x1, cc_, ALU.mult, ["S3", "cosC"], ["S1"])
            tt("dve", t2[:, 0:64], x2, ss_, ALU.mult, ["S3", "sinC"], ["S2"])
            tt("dve", kr[:, 0:64], t1[:, 0:64], t2[:, 0:64], ALU.subtract, ["S1", "S2"], ["S4"])
            tt("dve", t1[:, 64:128], x2, cc_, ALU.mult, ["S3", "cosC"], ["S1"])
            tt("dve", t2[:, 64:128], x1, ss_, ALU.mult, ["S3", "sinC"], ["S2"])
            tt("dve", kr[:, 64:128], t1[:, 64:128], t2[:, 64:128], ALU.add, ["S1", "S2"], ["S4"])
            tr(pb[6][:, 256:384], kr, ident32[:], ["S4", "ident32"], ["pb6"])
            act(kcT[:, ct * 128:(ct + 1) * 128], pb[6][:, 256:384], AF.Copy, ["pb6"], ["kcT"])

    qt = [P.sb([128, 512], BF16, f"nqt{i}") for i in range(2)]
    sg = [P.sb([128, 3, 512], F32, f"nsg{i}") for i in range(2)]
    pcs = [P.sb([128, 512], F32, f"npc{i}") for i in range(4)]
    qt32 = [P.sb([128, 512], F32, f"nqt32_{i}") for i in range(2)]
    pk = [P.sb([128, 512], BF16, f"npk{i}") for i in range(4)]
    rdc = P.sb([128, 512], F32, "rdc")
    wgt = P.sb([128, 512], F32, "wgt")
    acc = [P.sb([128, 512], F32, f"nacc{i}") for i in range(2)]
    score = P.sb([128, 128], F32, "score")
    sc2 = P.sb([128, 128], F32, "sc2")
    v8 = P.sb([128, 16], F32, "v8")
    sel = P.sb([128, 128], BF16, "sel")
    selT = P.sb([128, 128], BF16, "selT")
    nskeys = []

    def zq(row0, nrows, m):
        i = m // 2; h = m % 2
        return cx.nsq[i:i + 1, row0:row0 + nrows, h:h + 1, :].rearrange("a r h t -> (a r) (h t)")

    for m in range(nqt):
        pz = m % 2
        q_ = qt[pz]; kq = f"nqt{pz}"

        def q_loader(tile, key, rolled, m=m):
            for j in range(4):
                qb = j * 128
                cs_ = slice(j * 128, (j + 1) * 128)
                if not rolled:
                    P.dma("sync", tile[:, cs_], zq(qb, 128, m), writes=[key])
                else:
                    P.dma("sync", tile[0:64, cs_], zq(j * 128 + 64, 64, m), writes=[key])
                    P.dma("sync", tile[64:128, cs_], zq(qb, 64, m), writes=[key])
        rope_cm(q_loader, cosQ_d[:, m, :], sinQ_d[:, m, :], 512, 0, q_[:], kq, HD_SCALE, nj=4, dst32=qt32[pz][:], dkey32=f"nqt32_{pz}", alt=(m % 2 == 1))
        ksg = f"nsg{pz}"
        for br in range(3):
            for j in range(4):
                P.dma("sync" if (br + j) % 2 == 0 else "pool", sg[pz][:, br, j * 128:(j + 1) * 128],
                      zq(512 + j * 3 + br, 1, m).broadcast_to([128, 128]), writes=[ksg])
        act(sg[pz][:], sg[pz][:], AF.Sigmoid, [ksg], [ksg])
        kacc = f"nacc{pz}"

        def finish_branch(br, po, pd, kpo, kpd, first):
            ts("dve", rdc[:], pd[:, :], 1e-30, None, ALU.max, None, [kpd], ["rdc"])
            P.op("dve", lambda e: e.reciprocal(rdc[:], rdc[:]), ["rdc"], ["rdc"])
            tt("pool", wgt[:], rdc[:], sg[pz][:, br, :], ALU.mult, ["rdc", ksg], ["wgt"])
            if first:
                tt("dve", acc[pz][:], po[:, :], wgt[:], ALU.mult, [kpo, "wgt"], [kacc])
            else:
                tt("dve", wgt[:], po[:, :], wgt[:], ALU.mult, [kpo, "wgt"], ["wgt"])
                tt("pool", acc[pz][:], acc[pz][:], wgt[:], ALU.add, [kacc, "wgt"], [kacc])

        nct = m // 4 + 1
        for ct in range(nct):
            sb_ = pb[ct % 2]; ks_ = f"pb{ct % 2}"
            mm(sb_[:, :], kcT[:, ct * 128:(ct + 1) * 128], qt32[pz][:], True, True, ["kcT", f"nqt32_{pz}"], [ks_])
            act(pcs[ct][:], sb_[:, :], AF.Exp, [ks_], [f"npc{ct}"])
            p3c = pcs[ct][:].rearrange("p (j t) -> p j t", j=4)
            if ct == nct - 1:
                tt("dve", p3c, p3c, cmask[:, m % 4, :].unsqueeze(1).broadcast_to([128, 4, 128]), ALU.mult, [f"npc{ct}", "cmask"], [f"npc{ct}"])
            if ct == nct - 2:
                tt("dve", p3c, p3c, cmask2[:, m % 4, :].unsqueeze(1).broadcast_to([128, 4, 128]), ALU.mult, [f"npc{ct}", "cmask2"], [f"npc{ct}"])
            mm(pb[3][:, :], vc[:, ct, :], pcs[ct][:], ct == 0, ct == nct - 1, ["vc", f"npc{ct}"], ["pb3"])
            mm(pb[4][:, :], ones32[:], pcs[ct][:], ct == 0, ct == nct - 1, ["ones32", f"npc{ct}"], ["pb4"])
        finish_branch(0, pb[3], pb[4], "pb3", "pb4", True)
        for ct in range(nct):
            tt("dve", pcs[ct][:], pcs[ct][:], rdc[:], ALU.mult, [f"npc{ct}", "rdc"], [f"npc{ct}"])
        n_mm = 4 * nct; i_mm = 0
        for ct in range(nct):
            for j in range(4):
                mm(pb[7][:, 0:128], pcs[ct][:, j * 128:(j + 1) * 128], ov[:, ct, :], i_mm == 0, i_mm == n_mm - 1, [f"npc{ct}", "ov"], ["pb7"])
                i_mm += 1
        tt("dve", score[:], pb[7][:, 0:128], bonus[:, m, :], ALU.add, ["pb7", "bonus"], ["score"])
        P.op("dve", lambda e: e.max(out=v8[:, 0:8], in_=score[:]), ["score"], ["v8"])
        P.op("dve", lambda e: e.match_replace(out=sc2[:], in_to_replace=v8[:, 0:8], in_values=score[:], imm_value=-3e38), ["score", "v8"], ["sc2"])
        P.op("dve", lambda e: e.max(out=v8[:, 8:16], in_=sc2[:]), ["sc2"], ["v8"])
        ts("dve", v8[:, 15:16], v8[:, 15:16], -5e29, None, ALU.max, None, ["v8"], ["v8"])
        ts("dve", sel[:], score[:], v8[:, 15:16], None, ALU.is_ge, None, ["score", "v8"], ["sel"])
        p7b = pb[7][:, 256:512].bitcast(BF16)
        tr(p7b[:, 0:128], sel[:], ident[:], ["sel", "ident"], ["pb7"])
        act(selT[:], p7b[:, 0:128], AF.Copy, ["pb7"], ["selT"])
        nkt = 4 * m + 4
        LAG = 2

        def sel_a(kt):
            sb_ = pb[kt % 2]; ks_ = f"pb{kt % 2}"
            p_ = pk[kt % 4]; kp = f"npk{kt % 4}"
            mb_ = 2 if kt % 2 == 0 else 7
            msl = pb[mb_][:, 0:128]; mkey = f"pb{mb_}"
            mm(sb_[:, :], ksT[:, kt * 128:(kt + 1) * 128], q_[:], True, True, ["ksT", kq], [ks_])
            mm(msl, emat[:, kt * 128:(kt + 1) * 128], selT[:], True, True, ["emat", "selT"], [mkey])
            act(p_[:], sb_[:, :], AF.Exp, [ks_], [kp])
            p3 = p_[:].rearrange("p (j t) -> p j t", j=4)
            tt("dve", p3, p3, msl.unsqueeze(1).broadcast_to([128, 4, 128]), ALU.mult, [kp, mkey], [kp])
            if kt >= 4 * m:
                tt("dve", p3, p3, dmask[:, kt - 4 * m, :].unsqueeze(1).broadcast_to([128, 4, 128]), ALU.mult, [kp, "dmask"], [kp])

        def sel_b(kt):
            p_ = pk[kt % 4]; kp = f"npk{kt % 4}"
            mm(pb[5][:, :], vs[:, kt, :], p_[:], kt == 0, kt == nkt - 1, ["vs", kp], ["pb5"])
            mm(pb[6][:, :], onesb[:], p_[:], kt == 0, kt == nkt - 1, ["onesb", kp], ["pb6"])
        for kt in range(nkt + LAG):
            if kt < nkt:
                sel_a(kt)
            if kt >= LAG:
                sel_b(kt - LAG)
        finish_branch(1, pb[5], pb[6], "pb5", "pb6", False)
        wl = [w for w in range(8) if 4 * m - 4 + w >= 0]
        for iw, w in enumerate(wl):
            n_ = 4 * m - 4 + w
            sb_ = pb[iw % 2]; ks_ = f"pb{iw % 2}"
            p_ = pk[iw % 3]; kp = f"npk{iw % 3}"
            mm(sb_[:, :], kwT[:, n_ * 128:(n_ + 1) * 128], q_[:], True, True, ["kwT", kq], [ks_])
            act(p_[:], sb_[:, :], AF.Exp, [ks_], [kp])
            p3 = p_[:].rearrange("p (j t) -> p j t", j=4)
            tt("dve", p3, p3, wmask[:, w, :].unsqueeze(1).broadcast_to([128, 4, 128]), ALU.mult, [kp, "wmask"], [kp])
            mm(pb[3][:, :], vw[:, n_, :], p_[:], iw == 0, iw == len(wl) - 1, ["vw", kp], ["pb3"])
            mm(pb[4][:, :], onesb[:], p_[:], iw == 0, iw == len(wl) - 1, ["onesb", kp], ["pb4"])
        finish_branch(2, pb[3], pb[4], "pb3", "pb4", False)
        final.append(P.dma("sync", yns_d[:, m * 512:(m + 1) * 512], acc[pz][:], reads=[kacc], writes=[("nsaloc", m)]))
        nskeys.append(("nsaloc", m))
    final.append(P.cc("AllGather", cx.nsaloc[:, :], cx.nsafull[layer][:, :], reads=nskeys, writes=["nsafull"]))
    final.append(P.cc("AllGather", cx.bar_src[:, :], cx.bar_dst[:, :], reads=["nsafull"], writes=["bar"]))


_CONST_CACHE = {}


def _rope_tables(pos):
    inv = (10000.0 ** (-np.arange(64, dtype=np.float64) / 64.0)).astype(np.float32)
    ang = (pos.astype(np.float32)[None, :] * inv[:, None]).astype(np.float64)
    c = np.cos(ang); s = np.sin(ang)
    cos = np.concatenate([c, c], axis=0).astype(np.float32)
    sin = np.concatenate([-s, s], axis=0).astype(np.float32)
    return np.ascontiguousarray(cos), np.ascontiguousarray(sin)


def nsa_consts(r):
    key = ("nsa", r)
    if key in _CONST_CACHE:
        return _CONST_CACHE[key]
    cd = {}
    cd["cosT"], cd["sinT"] = _rope_tables(np.arange(T))
    cpos = 16 * np.arange(512) + 31
    inv = (10000.0 ** (-np.arange(64, dtype=np.float64) / 64.0)).astype(np.float32)
    angc = (cpos.astype(np.float32)[:, None] * inv[None, :]).astype(np.float64)
    cd["cosC"] = np.cos(angc).astype(np.float32).reshape(4, 128, 64).transpose(1, 0, 2)
    cd["sinC"] = np.sin(angc).astype(np.float32).reshape(4, 128, 64).transpose(1, 0, 2)
    qpos = ((4 * np.arange(NQT)[:, None] + r) * 128 + np.arange(128)[None, :]).reshape(-1)
    cq, sq = _rope_tables(qpos)
    cd["cosQ"] = cq.reshape(128, NQT, 128); cd["sinQ"] = sq.reshape(128, NQT, 128)
    cl = np.arange(128)[:, None, None]; b4 = np.arange(4)[None, :, None]; tl = np.arange(128)[None, None, :]
    cd["cmask"] = (16 * cl + 31 <= 512 * b4 + 128 * r + tl).astype(np.float32)
    cd["cmask2"] = (16 * cl + 31 <= 2048 + 512 * b4 + 128 * r + tl).astype(np.float32)
    dm = np.zeros((128, 4, 128), np.float32)
    kl = np.arange(128)[:, None]; tl2 = np.arange(128)[None, :]
    for jj in range(4):
        if jj < r:
            dm[:, jj, :] = 1.0
        elif jj == r:
            dm[:, jj, :] = (kl <= tl2)
    cd["dmask"] = dm
    wm = np.zeros((128, 8, 128), np.float32)
    for w in range(8):
        if w == r:
            wm[:, w, :] = (kl > tl2)
        elif w == r + 4:
            wm[:, w, :] = (kl <= tl2)
        elif r < w < r + 4:
            wm[:, w, :] = 1.0
    cd["wmask8"] = wm
    t_abs = (4 * np.arange(NQT)[None, :, None] + r) * 128 + np.arange(128)[:, None, None]
    cur = t_abs // 64
    s_ = np.arange(128)[None, None, :]
    elig = s_ <= cur
    forced = (s_ == 0) | (s_ == cur) | (s_ == cur - 1)
    cd["bonus"] = np.where(elig, np.where(forced, 1e4, 0.0), -1e30).astype(np.float32)
    cd["emat"] = (np.arange(T)[None, :] // 64 == np.arange(128)[:, None]).astype(np.float32)
    c_start = 16 * np.arange(512)
    s_start = 64 * np.arange(128)
    ovm = ((c_start[:, None] < s_start[None, :] + 64) & (c_start[:, None] + 32 > s_start[None, :])).astype(np.float32)
    cd["ov"] = ovm.reshape(4, 128, 128).transpose(1, 0, 2)
    cd["ident"] = np.eye(128, dtype=np.float32)
    cd["tri64"] = (np.arange(64)[:, None] <= np.arange(64)[None, :]).astype(np.float32)
    cd = {k: np.ascontiguousarray(v, dtype=np.float32) for k, v in cd.items()}
    _CONST_CACHE[key] = cd
    return cd


RUN_LAYERS = DEPTH
_FUSED = {}


DEBUG_STAGES = None


def build_fused(nl):
    PID_CACHE.clear()
    DYN_RR[0] = 0
    cx = Ctx()
    build_cast_phase(cx)
    build_token_phase(False, True, cx, 0)
    if DEBUG_STAGES is not None:
        if "rec" in DEBUG_STAGES:
            build_mixer_phase(("hg", "ml"), cx=cx, layer=0)
        if "nsa" in DEBUG_STAGES:
            build_mixer_phase(("ns",), cx=cx, layer=0)
        P = Prog(cx.nc)
        t = P.sb([128, 16, 1024], F32, "cp")
        P.dma("sync", t[:], cx.ext["xT"].rearrange("(c p) t -> p c t", p=128), writes=["cp"])
        ev = P.dma("sync", cx.ext["xT_out"].rearrange("(c p) t -> p c t", p=128), t[:], reads=["cp"])
        P.finish([ev])
        return cx
    for l in range(nl):
        build_mixer_phase(("hg", "ml"), cx=cx, layer=l)
        build_mixer_phase(("ns",), cx=cx, layer=l)
        build_token_phase(True, l != nl - 1, cx, l)
    return cx


def _vec(v, C):
    return np.asarray(v, np.float32).reshape(C, 128).T


def kernel(**inputs):
    inputs = {k: np.asarray(v) for k, v in inputs.items()}
    nl = RUN_LAYERS
    if nl not in _FUSED:
        _FUSED[nl] = build_fused(nl)
    cx = _FUSED[nl]
    f32 = np.float32
    xT = np.ascontiguousarray(inputs["x"][0].T.astype(f32))
    memT = np.ascontiguousarray(inputs["mem"][0].T.astype(f32))
    vpre = np.zeros((DEPTH, 128, 69), f32); vpost = np.zeros((DEPTH, 128, 64), f32)
    for l in range(DEPTH):
        bperm = np.zeros(53 * 128, f32)
        bperm[:D_IN] = inputs["b_in"][l][PERM]
        vpre[l] = np.concatenate([_vec(inputs["norm_mix"][l], 16), bperm.reshape(53, 128).T], axis=1)
        vpost[l] = np.concatenate([
            _vec(inputs["norm_xattn"][l], 16), _vec(inputs["norm_mem"][l], 16), _vec(inputs["norm_mlp"][l], 16),
            _vec(inputs["hgrn_norm"][l], 4), _vec(inputs["mlstm_norm"][l], 4),
            _vec(inputs["xa_q_norm"][l], 4), _vec(inputs["xa_k_norm"][l], 4)], axis=1)
    wflat = {}
    for name, n in W_SPECS:
        w = inputs[name]
        if name == "w_in":
            w = w[:, :, PERM]
        wflat[name] = np.ascontiguousarray(w, dtype=f32).reshape(NCORE, 128, -1)
    lmask = np.zeros((DEPTH, 128, DEPTH), f32)
    for l in range(DEPTH):
        lmask[l, :, 1:l + 1] = 1.0
    kg = inputs["nsa_k_norm"].astype(f32); qg = inputs["nsa_q_norm"].astype(f32)
    nvec = np.zeros((DEPTH, 128, 8), f32)
    for l in range(DEPTH):
        nvec[l, :, 0] = qg[l]; nvec[l, :, 1] = np.roll(qg[l], 64); nvec[l, :, 2] = kg[l, 1]; nvec[l, :, 3] = np.roll(kg[l, 1], 64)
        nvec[l, :, 4] = kg[l, 2]; nvec[l, :, 5] = np.roll(kg[l, 2], 64)
    cmp_w = np.ascontiguousarray(inputs["nsa_cmp_w"].transpose(0, 1, 3, 2, 4).astype(f32))
    cmp_pe = np.ascontiguousarray(inputs["nsa_cmp_pos"].transpose(0, 1, 3, 2).astype(f32))
    kcg = np.ascontiguousarray(np.broadcast_to(kg[:, 0][:, None, :], (DEPTH, 128, 128)).astype(f32))
    maps = []
    for c in range(NCORE):
        hh = c // 2
        mp = dict(nsa_consts(c % 4))
        mp.update({"xT": np.ascontiguousarray(xT[:, c * TPC:(c + 1) * TPC]), "memT": memT, "vpost_all": vpost, "vpre_all": vpre,
                   "hg_logit": np.ascontiguousarray(inputs["hgrn_lb_logits"][:, hh * 128:(hh + 1) * 128].T.astype(f32)),
                   "hg_lmask": lmask, "nvec": nvec, "cmp_w": cmp_w, "cmp_pe": cmp_pe, "kcg": kcg})
        conv = inputs["mlstm_conv"].astype(f32)
        mp["ml_cw"] = np.ascontiguousarray(np.concatenate([conv[:, :, hh * 128:(hh + 1) * 128].transpose(0, 2, 1),
                                                           conv[:, :, 512 + hh * 128:512 + (hh + 1) * 128].transpose(0, 2, 1)], axis=2))
        for name, n in W_SPECS:
            mp[name] = wflat[name][c]
        maps.append(mp)
    maps = [{k: v for k, v in mp.items() if k in cx.ext} for mp in maps]
    res = run_bass_kernel_spmd(cx.nc, maps, core_ids=list(range(NCORE))).results
    xo = np.concatenate([np.asarray(r["xT_out"]) for r in res], axis=1)
    return np.ascontiguousarray(xo.T)[None].astype(np.float32)
```

```python
from contextlib import ExitStack
import math
import numpy as np
import ml_dtypes
import concourse.bass as bass
import concourse.mybir as mybir
from concourse.bass_utils import run_bass_kernel_spmd

F32 = mybir.dt.float32
BF16 = mybir.dt.bfloat16
AF = mybir.ActivationFunctionType
ALU = mybir.AluOpType
AX = mybir.AxisListType
NPBF = ml_dtypes.bfloat16

SEM_CAP = 30000
DMA_SLOTS = 12

D = 2048
T = 8192
DEPTH = 4
NCORE = 8
TPC = T // NCORE
TB = 512
D_IN = 6688
DFF = 8192
EPS = 1e-6
MEM = 256
O_HGQ, O_HGF, O_HGI, O_HGG = 0, 512, 1024, 1536
O_NSQ, O_KC, O_VC, O_KS, O_VS, O_KW, O_VW, O_GATE = 2048, 3072, 3328, 3584, 3840, 4096, 4352, 4608
O_MLQ, O_MLK, O_MLV, O_MLO, O_MLI, O_MLF = 4632, 5144, 5656, 6168, 6680, 6684
PERM = np.concatenate([np.arange(0, 4608), np.arange(4632, 6680), np.arange(4608, 4632), np.arange(6680, 6688)])
Q_MLQ, Q_MLK, Q_MLV, Q_MLO, Q_GATE, Q_MLI, Q_MLF = 4608, 5120, 5632, 6144, 6656, 6680, 6684


class Prog:
    def __init__(self):
        self.nc = bass.Bass("TRN2", target_bir_lowering=False)
        self.es = ExitStack()
        self.streams = {e: [] for e in ("pe", "act", "dve", "pool", "sync")}
        self.cnt = {e: 0 for e in self.streams}
        self.sems = {}
        self.dslot_uses = {}
        self.dslot_last = {}
        self.dnext = {q: 0 for q in ("sync", "pool", "act")}
        self.lastw = {}
        self.readers = {}
        self.waited = {e: {} for e in self.streams}
        self.nalloc = 0
        self.rr = 0

    def sb(self, shape, dtype, name=None):
        self.nalloc += 1
        name = f"sb{self.nalloc}_{name or ''}"
        return self.es.enter_context(self.nc.sbuf_tensor(name, list(shape), dtype))

    def ps(self, shape, dtype=F32, name=None):
        self.nalloc += 1
        name = f"ps{self.nalloc}_{name or ''}"
        return self.es.enter_context(self.nc.psum_tensor(name, list(shape), dtype))

    def dram(self, name, shape, dtype, kind):
        return self.nc.dram_tensor(name, list(shape), dtype, kind=kind).ap()

    def _sem(self, key):
        if key not in self.sems:
            self.sems[key] = self.es.enter_context(self.nc.semaphore(f"s_{key[0]}_{key[1]}"))
        return self.sems[key]

    def _deps(self, reads, writes):
        evs = []
        for k in reads:
            if k in self.lastw:
                evs.append(self.lastw[k])
        for k in writes:
            if k in self.lastw:
                evs.append(self.lastw[k])
            evs.extend(self.readers.get(k, ()))
        return evs

    def _commit(self, ev, reads, writes):
        for k in reads:
            self.readers.setdefault(k, []).append(ev)
        for k in writes:
            self.lastw[k] = ev
            self.readers[k] = []

    def _waits_for(self, eng, evs):
        need = {}
        for (sk, v) in evs:
            if sk[0] == "pe" and eng == "pe":
                continue
            if self.waited[eng].get(sk, 0) >= v:
                continue
            if need.get(sk, 0) < v:
                need[sk] = v
        for sk, v in need.items():
            self.waited[eng][sk] = v
        return list(need.items())

    def op(self, eng, fn, reads=(), writes=()):
        reads = list(reads); writes = list(writes)
        evs = self._deps(reads, writes)
        waits = self._waits_for(eng, evs)
        n = self.cnt[eng]
        self.cnt[eng] = n + 1
        sk = (eng, n // SEM_CAP)
        ev = (sk, (n % SEM_CAP) + 1)
        self._sem(sk)
        self.streams[eng].append(("op", fn, waits, sk))
        self._commit(ev, reads, writes)
        return ev

    def dma(self, q, out, in_, reads=(), writes=(), **kw):
        reads = list(reads); writes = list(writes)
        slot = self.dnext[q] % DMA_SLOTS
        self.dnext[q] += 1
        dk = ("dma_" + q, slot)
        evs = self._deps(reads, writes)
        if dk in self.dslot_last:
            evs.append(self.dslot_last[dk])
        waits = self._waits_for(q, evs)
        uses = self.dslot_uses.get(dk, 0) + 1
        self.dslot_uses[dk] = uses
        self._sem(dk)
        ev = (dk, 16 * uses)
        self.dslot_last[dk] = ev
        self.streams[q].append(("dma", (out, in_, kw), waits, dk))
        self._commit(ev, reads, writes)
        return ev

    def custom(self, eng, fn, inc, reads=(), writes=()):
        reads = list(reads); writes = list(writes)
        evs = self._deps(reads, writes)
        waits = self._waits_for(eng, evs)
        self.ncustom = getattr(self, "ncustom", 0) + 1
        dk = ("cust", self.ncustom)
        self._sem(dk)
        ev = (dk, inc)
        self.streams[eng].append(("custom", (fn, inc), waits, dk))
        self._commit(ev, reads, writes)
        return ev

    def cc(self, kind, src, dst, reads=(), writes=()):
        return self.custom("pool", lambda e: e.collective_compute(kind, ALU.bypass, replica_groups=[list(range(NCORE))],
                                                                   ins=[src], outs=[dst]), 16, reads, writes)

    def finish(self, final_events=()):
        nc = self.nc
        fin_waits = self._waits_for("sync", list(final_events))
        self.streams["sync"].append(("waitonly", None, fin_waits, None))
        handles = {"pe": "tensor", "act": "scalar", "dve": "vector", "pool": "gpsimd", "sync": "sync"}
        with nc.Block() as block:
            for eng, hname in handles.items():
                stream = self.streams[eng]
                if not stream:
                    continue

                def body(e, stream=stream):
                    for kind, payload, waits, sk in stream:
                        for wsk, v in waits:
                            e.wait_ge(self.sems[wsk], v)
                        if kind == "op":
                            payload(e).then_inc(self.sems[sk], 1)
                        elif kind == "dma":
                            out, in_, kw = payload
                            e.dma_start(out=out, in_=in_, **kw).then_inc(self.sems[sk], 16)
                        elif kind == "custom":
                            fn, inc = payload
                            fn(e).then_inc(self.sems[sk], inc)

                getattr(block, hname)(body)
        self.es.close()
        return nc


def build_token_phase(do_post, do_pre):
    P = Prog()
    nc = P.nc
    NB = TPC // TB
    din = {}

    def inp(name, shape, dt=F32):
        din[name] = P.dram(name, shape, dt, "ExternalInput")
        return din[name]

    xT_d = inp("xT", [D, TPC])
    if do_post:
        ohg_d = inp("ohgT", [512, TPC]); oml_d = inp("omlT", [512, TPC]); yns_d = inp("ynsT", [1024, TPC])
        g_d = inp("gT", [512, TPC]); op_d = inp("opT", [512, TPC])
        memT_d = inp("memT", [D, MEM])
        wout_d = inp("w_out", [D, D], BF16); wq_d = inp("wq", [D, D], BF16); wk_d = inp("wk", [D, D], BF16)
        wv_d = inp("wv", [D, D], BF16); wo_d = inp("wo", [D, D], BF16)
        w1_d = inp("w1", [D, DFF], BF16); w2_d = inp("w2", [DFF, D], BF16)
        vpost_d = inp("vpost", [128, 16 * 3 + 4 * 4])
        xout_d = P.dram("xT_out", [D, TPC], F32, "ExternalOutput")
    if do_pre:
        win_d = inp("w_in", [D, D_IN], BF16)
        vpre_d = inp("vpre", [128, 16 + 53])
        z_d = P.dram("zT_out", [D_IN, TPC], F32, "ExternalOutput")

    xT = P.sb([128, 16, TB], F32, "xT")
    hT = P.sb([128, 16, TB], BF16, "hT")
    big = P.sb([128, 16, TB], BF16, "big")
    big2 = P.sb([128, 16, TB], BF16, "big2")
    NWB = 2
    wts = [P.sb([128, 16, 512], BF16, f"wt{i}") for i in range(NWB)]
    o32 = P.sb([128, 4, TB], F32, "o32")
    g32 = P.sb([128, 4, TB], F32, "g32")
    pT = P.sb([128, 2, TB], BF16, "pT")
    rden = P.sb([128, TB], F32, "rden")
    sq = [P.sb([128, TB], F32, f"sq{i}") for i in range(2)]
    rstd = P.sb([128, TB], F32, "rstd")
    stage = [P.sb([128, TB], F32, f"stage{i}") for i in range(3)]
    ones32 = P.sb([128, 128], F32, "ones32")
    onesb = P.sb([128, 128], BF16, "onesb")
    P.op("dve", lambda e: e.memset(ones32[:], 1.0), writes=["ones32"])
    P.op("dve", lambda e: e.memset(onesb[:], 1.0), writes=["onesb"])
    gp = [P.ps([128, TB], F32, f"gp{i}") for i in range(4)]
    st = P.ps([128, TB], F32, "st")
    ap_ = [P.ps([128, TB], F32, f"ap{i}") for i in range(3)]

    wi = [0]

    def next_w():
        i = wi[0] % NWB
        wi[0] += 1
        return wts[i], ("wt", i)

    evi = [0]

    def norm_stats(src_fn, skeys, C, Dn, out_rstd=None, out_key="rstd"):
        out_rstd = rstd if out_rstd is None else out_rstd
        for c in range(C):
            b = c % 2
            P.op("act", lambda e, c=c, b=b: e.activation(out=sq[b][:], in_=src_fn(c), func=AF.Square),
                 reads=[skeys[c]], writes=[("sq", b)])
            P.op("pe", lambda e, c=c, b=b: e.matmul(st[:], lhsT=ones32[:], rhs=sq[b][:], start=(c == 0), stop=(c == C - 1)),
                 reads=[("sq", b), "ones32"], writes=["st"])
        P.op("act", lambda e: e.activation(out=out_rstd[:], in_=st[:], func=AF.Sqrt, scale=1.0 / Dn, bias=epsb[:, 0:1]),
             reads=["st", "epsb"], writes=[out_key])
        P.op("dve", lambda e: e.reciprocal(out_rstd[:], out_rstd[:]), reads=[out_key], writes=[out_key])

    epsb = P.sb([128, 1], F32, "epsb")
    P.op("dve", lambda e: e.memset(epsb[:], EPS), writes=["epsb"])

    def gemm(Wd, K, N, rhs_fn, rkey_fn, evac, ntok=TB):
        KG = K // 2048
        for n0 in range(0, N, 512):
            nw = min(512, N - n0)
            nj = (nw + 127) // 128
            for kg in range(KG):
                wt, wkey = next_w()
                src = Wd[kg * 2048:(kg + 1) * 2048, n0:n0 + nw].rearrange("(c p) n -> p c n", p=128)
                P.dma("sync", wt[:, 0:8, 0:nw], src[:, 0:8, :], writes=[(wkey, 0)])
                P.dma("pool", wt[:, 8:16, 0:nw], src[:, 8:16, :], writes=[(wkey, 1)])
                for j in range(nj):
                    cw = min(128, nw - j * 128)
                    for kc in range(16):
                        first = (kg == 0 and kc == 0)
                        last = (kg == KG - 1 and kc == 15)
                        P.op("pe", lambda e, wt=wt, j=j, cw=cw, kc=kc, kg=kg, first=first, last=last:
                             e.matmul(gp[j][0:cw, 0:ntok], lhsT=wt[:, kc, j * 128:j * 128 + cw], rhs=rhs_fn(kg * 16 + kc),
                                      start=first, stop=last),
                             reads=[(wkey, kc // 8), rkey_fn(kg * 16 + kc)], writes=[("gp", j)])
            for j in range(nj):
                cw = min(128, nw - j * 128)
                evac(n0 // 128 + j, cw, gp[j], ("gp", j))

    def load_cols(dst, dkey, src_d, nrows_chunks, b, q="sync"):
        src = src_d.rearrange("(c p) t -> p c t", p=128)
        P.dma(q, dst[:, 0:nrows_chunks, :], src[:, :, b * TB:(b + 1) * TB], writes=[(dkey, c) for c in range(nrows_chunks)])

    if do_post:
        vpost = P.sb([128, 64], F32, "vpost")
        P.dma("sync", vpost[:], vpost_d[:, :], writes=["vpost"])
        G_XA, G_MEM, G_MLP, G_HG, G_ML, G_XQ, G_XK = 0, 16, 32, 48, 52, 56, 60
        memT = xT[:, :, 0:MEM]
        memn = hT[:, :, 0:MEM]
        kT32 = xT[:, :, 0:MEM]
        kT = P.sb([128, 16, MEM], BF16, "kT")
        vtok = P.sb([128, 2, D], BF16, "vtok")
        P.dma("sync", memT, memT_d.rearrange("(c p) m -> p c m", p=128), writes=[("xT", c) for c in range(16)])
        for c in range(16):
            b = c % 2
            P.op("act", lambda e, c=c, b=b: e.activation(out=sq[b][:, 0:MEM], in_=memT[:, c, :], func=AF.Square),
                 reads=[("xT", c)], writes=[("sq", b)])
            P.op("pe", lambda e, c=c, b=b: e.matmul(st[:, 0:MEM], lhsT=ones32[:], rhs=sq[b][:, 0:MEM], start=(c == 0), stop=(c == 15)),
                 reads=[("sq", b), "ones32"], writes=["st"])
        P.op("act", lambda e: e.activation(out=rstd[:, 0:MEM], in_=st[:, 0:MEM], func=AF.Sqrt, scale=1.0 / D, bias=epsb[:, 0:1]),
             reads=["st", "epsb"], writes=["rstd"])
        P.op("dve", lambda e: e.reciprocal(rstd[:, 0:MEM], rstd[:, 0:MEM]), reads=["rstd"], writes=["rstd"])
        for c in range(16):
            P.op("dve", lambda e, c=c: e.scalar_tensor_tensor(out=memn[:, c, :], in0=memT[:, c, :], scalar=vpost[:, G_MEM + c:G_MEM + c + 1],
                                                               in1=rstd[:, 0:MEM], op0=ALU.mult, op1=ALU.mult),
                 reads=[("xT", c), "rstd", "vpost"], writes=[("hT", c)])

        def evac_k(j, cw, ps, pkey):
            P.op("act", lambda e: e.activation(out=kT32[:, j, :], in_=ps[:, 0:MEM], func=AF.Copy), reads=[pkey], writes=[("xT", j)])
        gemm(wk_d, D, D, lambda c: memn[:, c, :], lambda c: ("hT", c), evac_k, ntok=MEM)
        for hh in range(4):
            for c4 in range(4):
                c = hh * 4 + c4; b = c % 2
                P.op("act", lambda e, c=c, b=b: e.activation(out=sq[b][:, 0:MEM], in_=kT32[:, c, :], func=AF.Square),
                     reads=[("xT", c)], writes=[("sq", b)])
                P.op("pe", lambda e, c4=c4, b=b: e.matmul(st[:, 0:MEM], lhsT=ones32[:], rhs=sq[b][:, 0:MEM], start=(c4 == 0), stop=(c4 == 3)),
                     reads=[("sq", b), "ones32"], writes=["st"])
            P.op("act", lambda e: e.activation(out=rstd[:, 0:MEM], in_=st[:, 0:MEM], func=AF.Sqrt, scale=1.0 / 512, bias=epsb[:, 0:1]),
                 reads=["st", "epsb"], writes=["rstd"])
            P.op("dve", lambda e: e.reciprocal(rstd[:, 0:MEM], rstd[:, 0:MEM]), reads=["rstd"], writes=["rstd"])
            for c4 in range(4):
                c = hh * 4 + c4
                P.op("dve", lambda e, c=c, c4=c4: e.scalar_tensor_tensor(out=kT[:, c, :], in0=kT32[:, c, :], scalar=vpost[:, G_XK + c4:G_XK + c4 + 1],
                                                                         in1=rstd[:, 0:MEM], op0=ALU.mult, op1=ALU.mult),
                     reads=[("xT", c), "rstd", "vpost"], writes=[("kT", c)])
        for n0 in range(0, D, 512):
            wt, wkey = next_w()
            src = wv_d[:, n0:n0 + 512].rearrange("(c p) n -> p c n", p=128)
            P.dma("sync", wt[:, 0:8, :], src[:, 0:8, :], writes=[(wkey, 0)])
            P.dma("pool", wt[:, 8:16, :], src[:, 8:16, :], writes=[(wkey, 1)])
            for mt in range(2):
                for kc in range(16):
                    P.op("pe", lambda e, wt=wt, mt=mt, kc=kc: e.matmul(gp[mt][:, :], lhsT=memn[:, kc, mt * 128:(mt + 1) * 128], rhs=wt[:, kc, :],
                                                                      start=(kc == 0), stop=(kc == 15)),
                         reads=[(wkey, kc // 8), ("hT", kc)], writes=[("gp", mt)])
                P.op("act", lambda e, mt=mt, n0=n0: e.activation(out=vtok[:, mt, n0:n0 + 512], in_=gp[mt][:, :], func=AF.Copy),
                     reads=[("gp", mt)], writes=[("vtok", mt, n0)])
        vkeys = [("vtok", mt, n0) for mt in range(2) for n0 in range(0, D, 512)]
    if do_pre:
        vpre = P.sb([128, 69], F32, "vpre")
        P.dma("sync", vpre[:], vpre_d[:, :], writes=["vpre"])

    final_evs = []
    for b in range(NB):
        load_cols(xT, "xT", xT_d, 16, b)
        if do_post:
            yT = big2
            for which in range(2):
                od = ohg_d if which == 0 else oml_d
                gd = g_d if which == 0 else op_d
                goff = G_HG if which == 0 else G_ML
                cbase = 0 if which == 0 else 12
                load_cols(o32, "o32", od, 4, b)
                load_cols(g32, "g32", gd, 4, b, q="pool")
                for hh in range(4):
                    norm_stats(lambda c, hh=hh: o32[:, hh, :], [("o32", hh)], 1, 128.0)
                    P.op("act", lambda e, hh=hh, which=which: e.activation(out=g32[:, hh, :], in_=g32[:, hh, :],
                                                                            func=(AF.Silu if which == 0 else AF.Sigmoid)),
                         reads=[("g32", hh)], writes=[("g32", hh)])
                    P.op("dve", lambda e, hh=hh, goff=goff: e.scalar_tensor_tensor(out=o32[:, hh, :], in0=o32[:, hh, :], scalar=vpost[:, goff + hh:goff + hh + 1],
                                                                                    in1=rstd[:], op0=ALU.mult, op1=ALU.mult),
                         reads=[("o32", hh), "rstd", "vpost"], writes=[("o32", hh)])
                    P.op("dve", lambda e, hh=hh, cbase=cbase: e.tensor_tensor(out=yT[:, cbase + hh, :], in0=o32[:, hh, :], in1=g32[:, hh, :], op=ALU.mult),
                         reads=[("o32", hh), ("g32", hh)], writes=[("big2", cbase + hh)])
            for half in range(2):
                load_cols(o32, "o32", yns_d[half * 512:(half + 1) * 512, :], 4, b)
                for hh in range(4):
                    P.op("act", lambda e, hh=hh, half=half: e.activation(out=yT[:, 4 + half * 4 + hh, :], in_=o32[:, hh, :], func=AF.Copy),
                         reads=[("o32", hh)], writes=[("big2", 4 + half * 4 + hh)])

            def evac_add(j, cw, ps, pkey):
                P.op("dve", lambda e: e.tensor_tensor(out=xT[:, j, :], in0=ps[:, :], in1=xT[:, j, :], op=ALU.add),
                     reads=[pkey, ("xT", j)], writes=[("xT", j)])
            gemm(wout_d, D, D, lambda c: yT[:, c, :], lambda c: ("big2", c), evac_add)

            def norm_x(goff_tab, tab):
                norm_stats(lambda c: xT[:, c, :], [("xT", c) for c in range(16)], 16, float(D))
                for c in range(16):
                    P.op("dve", lambda e, c=c: e.scalar_tensor_tensor(out=hT[:, c, :], in0=xT[:, c, :], scalar=tab[:, goff_tab + c:goff_tab + c + 1],
                                                                       in1=rstd[:], op0=ALU.mult, op1=ALU.mult),
                         reads=[("xT", c), "rstd", "vpost" if tab is vpost else "vpre"], writes=[("hT", c)])
            norm_x(G_XA, vpost)

            q32 = [o32[:, i, :] for i in range(4)]
            qn = big
            oxT = big2

            def evac_q(j, cw, ps, pkey):
                c4 = j % 4
                P.op("act", lambda e: e.activation(out=q32[c4], in_=ps[:, :], func=AF.Copy), reads=[pkey], writes=[("o32", c4)])
                if c4 == 3:
                    hh = j // 4
                    norm_stats(lambda c: q32[c], [("o32", c) for c in range(4)], 4, 512.0)
                    for c in range(4):
                        P.op("dve", lambda e, c=c, hh=hh: e.scalar_tensor_tensor(out=qn[:, hh * 4 + c, :], in0=q32[c], scalar=vpost[:, G_XQ + c:G_XQ + c + 1],
                                                                                 in1=rstd[:], op0=ALU.mult, op1=ALU.mult),
                             reads=[("o32", c), "rstd", "vpost"], writes=[("big", hh * 4 + c)])
            gemm(wq_d, D, D, lambda c: hT[:, c, :], lambda c: ("hT", c), evac_q)

            sc = 512.0 ** -0.5
            for hh in range(4):
                for mt in range(2):
                    for c4 in range(4):
                        c = hh * 4 + c4
                        P.op("pe", lambda e, mt=mt, c=c, c4=c4: e.matmul(ap_[mt][:, :], lhsT=kT[:, c, mt * 128:(mt + 1) * 128], rhs=qn[:, c, :],
                                                                         start=(c4 == 0), stop=(c4 == 3)),
                             reads=[("kT", c), ("big", c)], writes=[("ap", mt)])
                    P.op("act", lambda e, mt=mt: e.activation(out=pT[:, mt, :], in_=ap_[mt][:, :], func=AF.Exp, scale=sc),
                         reads=[("ap", mt)], writes=[("pT", mt)])
                for mt in range(2):
                    P.op("pe", lambda e, mt=mt: e.matmul(ap_[2][:, :], lhsT=onesb[:], rhs=pT[:, mt, :], start=(mt == 0), stop=(mt == 1)),
                         reads=[("pT", mt), "onesb"], writes=[("ap", 2)])
                P.op("dve", lambda e: e.reciprocal(rden[:], ap_[2][:, :]), reads=[("ap", 2)], writes=["rden"])
                for c4 in range(4):
                    c = hh * 4 + c4
                    for mt in range(2):
                        P.op("pe", lambda e, mt=mt, c=c, c4=c4: e.matmul(gp[c4][:, :], lhsT=vtok[:, mt, c * 128:(c + 1) * 128], rhs=pT[:, mt, :],
                                                                         start=(mt == 0), stop=(mt == 1)),
                             reads=[("pT", mt)] + vkeys, writes=[("gp", c4)])
                    P.op("dve", lambda e, c=c, c4=c4: e.tensor_tensor(out=oxT[:, c, :], in0=gp[c4][:, :], in1=rden[:], op=ALU.mult),
                         reads=[("gp", c4), "rden"], writes=[("big2", c)])
            gemm(wo_d, D, D, lambda c: oxT[:, c, :], lambda c: ("big2", c), evac_add)
            norm_x(G_MLP, vpost)
            hid = big
            for half in range(4):
                def evac_h(j, cw, ps, pkey, half=half):
                    sb_ = stage[j % 3]
                    P.op("act", lambda e: e.activation(out=sb_[:], in_=ps[:, :], func=AF.Relu), reads=[pkey], writes=[("stage", j % 3)])
                    P.op("pool", lambda e: e.tensor_tensor(out=hid[:, j, :], in0=sb_[:], in1=sb_[:], op=ALU.mult),
                         reads=[("stage", j % 3)], writes=[("big", j)])
                gemm(w1_d[:, half * 2048:(half + 1) * 2048], D, 2048, lambda c: hT[:, c, :], lambda c: ("hT", c), evac_h)
                gemm(w2_d[half * 2048:(half + 1) * 2048, :], 2048, D, lambda c: hid[:, c, :], lambda c: ("big", c), evac_add)
            if not do_pre:
                dst = xout_d.rearrange("(c p) t -> p c t", p=128)
                ev = P.dma("sync", dst[:, :, b * TB:(b + 1) * TB], xT[:, :, :], reads=[("xT", c) for c in range(16)])
                final_evs.append(ev)
        if do_pre:
            norm_stats(lambda c: xT[:, c, :], [("xT", c) for c in range(16)], 16, float(D))
            for c in range(16):
                P.op("dve", lambda e, c=c: e.scalar_tensor_tensor(out=hT[:, c, :], in0=xT[:, c, :], scalar=vpre[:, c:c + 1],
                                                                   in1=rstd[:], op0=ALU.mult, op1=ALU.mult),
                     reads=[("xT", c), "rstd", "vpre"], writes=[("hT", c)])

            def evac_z(j, cw, ps, pkey, b=b):
                sb_ = stage[j % 3]
                P.op("act", lambda e: e.activation(out=sb_[0:cw, :], in_=ps[0:cw, :], func=AF.Identity, bias=vpre[0:cw, 16 + j:17 + j]),
                     reads=[pkey, "vpre"], writes=[("stage", j % 3)])
                ev = P.dma("sync", z_d[j * 128:j * 128 + cw, b * TB:(b + 1) * TB], sb_[0:cw, :], reads=[("stage", j % 3)])
                final_evs.append(ev)
            gemm(win_d, D, D_IN, lambda c: hT[:, c, :], lambda c: ("hT", c), evac_z)
            if do_post:
                dst = xout_d.rearrange("(c p) t -> p c t", p=128)
                ev = P.dma("sync", dst[:, :, b * TB:(b + 1) * TB], xT[:, :, :], reads=[("xT", c) for c in range(16)])
                final_evs.append(ev)
    return P.finish(final_evs)


W_SPECS = [("w_in", D * D_IN), ("w_out", D * D), ("xa_wq", D * D), ("xa_wk", D * D), ("xa_wv", D * D), ("xa_wo", D * D),
           ("mlp_w1", D * DFF), ("mlp_w2", D * DFF)]


def build_cast_phase():
    P = Prog()
    CH = 8192
    bufs = [P.sb([128, CH], BF16, f"cb{i}") for i in range(4)]
    bi = 0
    evs = []
    for name, n in W_SPECS:
        m = n * DEPTH // NCORE // 128
        src = P.dram(name, [128, m], F32, "ExternalInput")
        dst = P.dram(name + "_b", [128, m], BF16, "ExternalOutput")
        for c0 in range(0, m, CH):
            cw = min(CH, m - c0)
            bt = bufs[bi % 4]; key = ("cb", bi % 4); bi += 1
            P.dma("pool", bt[:, 0:cw], src[:, c0:c0 + cw], writes=[key])
            evs.append(P.dma("sync", dst[:, c0:c0 + cw], bt[:, 0:cw], reads=[key]))
    return P.finish(evs)


def run_cast(inputs):
    nc = build_cast_phase()
    in_maps = []
    for c in range(NCORE):
        mp = {}
        for name, n in W_SPECS:
            w = inputs[name]
            if name == "w_in":
                w = w[:, :, PERM]
            flat = np.ascontiguousarray(w).reshape(NCORE, 128, -1)
            mp[name] = flat[c]
        in_maps.append(mp)
    res = run_bass_kernel_spmd(nc, in_maps, core_ids=list(range(NCORE)))
    out = {}
    for name, n in W_SPECS:
        full = np.stack([np.asarray(res.results[c][name + "_b"]) for c in range(NCORE)], axis=0)
        shp = list(inputs[name].shape)
        out[name] = full.reshape(shp)
    return out


SBK = 512
NCHK = SBK // 64
NSB = T // SBK
NQT = 16
KWN = 65 * 128
KWP = 416
HD_SCALE = 128.0 ** -0.5


def build_mixer_phase(parts=("hg", "ml", "ns"), nsb=NSB, nqt=NQT):
    P = Prog()
    din = {}

    def inp(name, shape, dt=F32):
        din[name] = P.dram(name, shape, dt, "ExternalInput")
        return din[name]

    def act(out, in_, func, r, w, **kw):
        return P.op("act", lambda e: e.activation(out=out, in_=in_, func=func, **kw), r, w)

    def tt(eng, out, a, b, op, r, w):
        return P.op(eng, lambda e: e.tensor_tensor(out=out, in0=a, in1=b, op=op), r, w)

    def ts(eng, out, a, s1, s2, op0, op1, r, w):
        if op1 is None:
            return P.op(eng, lambda e: e.tensor_scalar(out=out, in0=a, scalar1=s1, scalar2=None, op0=op0), r, w)
        return P.op(eng, lambda e: e.tensor_scalar(out=out, in0=a, scalar1=s1, scalar2=s2, op0=op0, op1=op1), r, w)

    def stt(out, a, s, b, op0, op1, r, w):
        return P.op("dve", lambda e: e.scalar_tensor_tensor(out=out, in0=a, scalar=s, in1=b, op0=op0, op1=op1), r, w)

    def mm(out, lhsT, rhs, start, stop, r, w):
        return P.op("pe", lambda e: e.matmul(out, lhsT=lhsT, rhs=rhs, start=start, stop=stop), r, w)

    def tr(out, in_, ident_ap, r, w):
        return P.op("pe", lambda e: e.transpose(out, in_, ident_ap), r, w)

    consts_d = inp("ident", [128, 128])
    tri_d = inp("tri64", [64, 64])
    ident = P.sb([128, 128], BF16, "ident")
    ident32 = P.sb([128, 128], F32, "ident32")
    tri = P.sb([64, 64], BF16, "tri")
    P.dma("pool", ident[:], consts_d[:, :], writes=["ident"])
    P.dma("sync", ident32[:], consts_d[:, :], writes=["ident32"])
    P.dma("pool", tri[:], tri_d[:, :], writes=["tri"])
    onesb = P.sb([128, 128], BF16, "onesb")
    ones32 = P.sb([128, 128], F32, "ones32")
    one1 = P.sb([128, 1], F32, "one1")
    epsb = P.sb([128, 1], F32, "epsb")
    P.op("dve", lambda e: e.memset(onesb[:], 1.0), writes=["onesb"])
    P.op("dve", lambda e: e.memset(ones32[:], 1.0), writes=["ones32"])
    P.op("dve", lambda e: e.memset(one1[:], 1.0), writes=["one1"])
    P.op("dve", lambda e: e.memset(epsb[:], EPS), writes=["epsb"])
    rmask = P.sb([128, SBK], F32, "rmask")
    nmask = P.sb([128, SBK], F32, "nmask")
    P.op("dve", lambda e: e.memset(rmask[:], 1.0), writes=["rmask"])
    P.op("dve", lambda e: e.memset(rmask[:, 0::64], 0.0), writes=["rmask"])
    P.op("dve", lambda e: e.memset(nmask[:], 0.0), writes=["nmask"])
    P.op("dve", lambda e: e.memset(nmask[:, 0::64], -1e30), writes=["nmask"])

    pb = [P.ps([128, 512], F32, f"pb{i}") for i in range(8)]
    S = [P.sb([128, SBK], F32, f"S{i}") for i in range(8)]
    final = []

    def c3(ap):
        return ap.rearrange("p (c s) -> p c s", s=64)

    if "hg" in parts:
        hq_d = inp("hg_qT", [128, T]); hf_d = inp("hg_fT", [128, T]); hv_d = inp("hg_v", [64, 128, 64])
        hlog_d = inp("hg_logit", [128, DEPTH]); hlm_d = inp("hg_lmask", [128, DEPTH])
        ohg_d = P.dram("ohg", [64, T], F32, "ExternalOutput")
        hv = P.sb([64, 128, 64], BF16, "hv")
        P.dma("pool", hv[:], hv_d[:, :, :], writes=["hv"])
        lg = P.sb([128, DEPTH], F32, "lg"); lm = P.sb([128, DEPTH], F32, "lm")
        P.dma("sync", lg[:], hlog_d[:, :], writes=["lg"]); P.dma("sync", lm[:], hlm_d[:, :], writes=["lm"])
        lsm = P.sb([128, 8], F32, "lsm")
        P.op("dve", lambda e: e.tensor_reduce(out=lsm[:, 0:1], in_=lg[:], axis=AX.X, op=ALU.max), ["lg"], ["lsm"])
        ts("dve", lg[:], lg[:], lsm[:, 0:1], None, ALU.subtract, None, ["lg", "lsm"], ["lg"])
        act(lg[:], lg[:], AF.Exp, ["lg"], ["lg"])
        P.op("dve", lambda e: e.tensor_reduce(out=lsm[:, 1:2], in_=lg[:], axis=AX.X, op=ALU.add), ["lg"], ["lsm"])
        P.op("dve", lambda e: e.reciprocal(lsm[:, 1:2], lsm[:, 1:2]), ["lsm"], ["lsm"])
        tt("dve", lg[:], lg[:], lm[:], ALU.mult, ["lg", "lm"], ["lg"])
        P.op("dve", lambda e: e.tensor_reduce(out=lsm[:, 2:3], in_=lg[:], axis=AX.X, op=ALU.add), ["lg"], ["lsm"])
        tt("dve", lsm[:, 2:3], lsm[:, 2:3], lsm[:, 1:2], ALU.mult, ["lsm"], ["lsm"])
        ts("dve", lsm[:, 3:4], lsm[:, 2:3], -1.0, 1.0, ALU.mult, ALU.add, ["lsm"], ["lsm"])
        ts("dve", lsm[:, 4:5], lsm[:, 3:4], -1.0, None, ALU.mult, None, ["lsm"], ["lsm"])
        LB, OML, NOML = lsm[:, 2:3], lsm[:, 3:4], lsm[:, 4:5]

        st32 = P.sb([128, 64], F32, "hst32"); stbf = P.sb([128, 64], BF16, "hstbf")
        P.op("dve", lambda e: e.memset(st32[:], 0.0), writes=["hst32"])
        P.op("dve", lambda e: e.memset(stbf[:], 0.0), writes=["hstbf"])
        hq32 = [P.sb([128, SBK], F32, f"hq32_{i}") for i in range(2)]
        hf32 = [P.sb([128, SBK], F32, f"hf32_{i}") for i in range(2)]
        qtT = [P.sb([128, SBK], BF16, f"qtT{i}") for i in range(2)]
        ktT = [P.sb([128, SBK], BF16, f"ktT{i}") for i in range(2)]
        qbT = [P.sb([128, SBK], BF16, f"qbT{i}") for i in range(2)]
        k2T = [P.sb([128, SBK], BF16, f"k2T{i}") for i in range(2)]
        k2s = [P.sb([64, NCHK, 128], BF16, f"k2s{i}") for i in range(2)]
        STs = [P.sb([64, NCHK, 64], BF16, f"STs{i}") for i in range(2)]
        decs = [P.sb([128, NCHK], F32, f"decs{i}") for i in range(2)]
        osb = [P.sb([64, SBK], F32, f"osb{i}") for i in range(2)]
        A, B_, C_, D_, E_ = S[0], S[1], S[2], S[3], S[4]
        for sb in range(nsb):
            pz = sb % 2; t0 = sb * SBK
            q32 = hq32[pz]; f32 = hf32[pz]
            kq, kf = f"hq32_{pz}", f"hf32_{pz}"
            P.dma("sync", q32[:], hq_d[:, t0:t0 + SBK], writes=[kq])
            P.dma("sync", f32[:], hf_d[:, t0:t0 + SBK], writes=[kf])
            act(A[:], f32[:], AF.Sigmoid, [kf], ["A"])
            ts("dve", B_[:], A[:], OML, LB, ALU.mult, ALU.add, ["A", "lsm"], ["B"])
            ts("dve", B_[:], B_[:], 1e-20, None, ALU.max, None, ["B"], ["B"])
            act(B_[:], B_[:], AF.Ln, ["B"], ["B"])
            ts("pool", A[:], A[:], NOML, OML, ALU.mult, ALU.add, ["A", "lsm"], ["A"])
            P.op("dve", lambda e: e.tensor_tensor_scan(out=C_[:], data0=rmask[:], data1=B_[:], initial=0.0, op0=ALU.mult, op1=ALU.add),
                 ["rmask", "B"], ["C"])
            tt("dve", c3(D_[:]), c3(C_[:]), c3(C_[:])[:, :, 31:32].broadcast_to([128, NCHK, 64]), ALU.subtract, ["C"], ["D"])
            act(E_[:], D_[:], AF.Exp, ["D"], ["E"])
            tt("pool", qtT[pz][:], q32[:], E_[:], ALU.mult, [kq, "E"], [f"qtT{pz}"])
            act(E_[:], D_[:], AF.Exp, ["D"], ["E"], scale=-1.0)
            tt("dve", ktT[pz][:], A[:], E_[:], ALU.mult, ["A", "E"], [f"ktT{pz}"])
            act(E_[:], C_[:], AF.Exp, ["C"], ["E"])
            tt("pool", qbT[pz][:], q32[:], E_[:], ALU.mult, [kq, "E"], [f"qbT{pz}"])
            tt("dve", c3(D_[:]), c3(C_[:])[:, :, 63:64].broadcast_to([128, NCHK, 64]), c3(C_[:]), ALU.subtract, ["C"], ["D"])
            act(D_[:], D_[:], AF.Exp, ["D"], ["D"])
            tt("dve", k2T[pz][:], A[:], D_[:], ALU.mult, ["A", "D"], [f"k2T{pz}"])
            act(decs[pz][:], C_[:, 63::64], AF.Exp, ["C"], [f"decs{pz}"])
            for c in range(NCHK):
                mm(pb[0][0:64, c * 64:(c + 1) * 64], ktT[pz][:, c * 64:(c + 1) * 64], qtT[pz][:, c * 64:(c + 1) * 64], True, True,
                   [f"ktT{pz}", f"qtT{pz}"], ["pb0"])
            tt("dve", STs[pz][:], c3(pb[0][0:64, :]), tri[:].unsqueeze(1).broadcast_to([64, NCHK, 64]), ALU.mult, ["pb0", "tri"], [f"STs{pz}"])
            pb1b = pb[1][0:64, :].bitcast(BF16)
            for c in range(NCHK):
                tr(pb1b[:, c * 128:(c + 1) * 128], k2T[pz][:, c * 64:(c + 1) * 64], ident[:], [f"k2T{pz}", "ident"], ["pb1"])
            act(k2s[pz][:], pb1b.rearrange("p (c d) -> p c d", d=128), AF.Copy, ["pb1"], [f"k2s{pz}"])
            for c in range(NCHK):
                ch = sb * NCHK + c
                mm(pb[2][0:64, c * 64:(c + 1) * 64], hv[:, ch, :], STs[pz][:, c, :], True, False, ["hv", f"STs{pz}"], ["pb2"])
                mm(pb[2][0:64, c * 64:(c + 1) * 64], stbf[:], qbT[pz][:, c * 64:(c + 1) * 64], False, True, ["hstbf", f"qbT{pz}"], ["pb2"])
                mm(pb[3][:, 0:64], k2s[pz][:, c, :], hv[:, ch, :], True, True, [f"k2s{pz}", "hv"], ["pb3"])
                stt(st32[:], st32[:], decs[pz][:, c:c + 1], pb[3][:, 0:64], ALU.mult, ALU.add, ["hst32", f"decs{pz}", "pb3"], ["hst32"])
                act(stbf[:], st32[:], AF.Copy, ["hst32"], ["hstbf"])
            act(osb[pz][:], pb[2][0:64, :], AF.Copy, ["pb2"], [f"osb{pz}"])
            final.append(P.dma("sync", ohg_d[:, t0:t0 + SBK], osb[pz][:], reads=[f"osb{pz}"]))

    if "ml" in parts:
        mq_d = inp("ml_qT", [128, T + 3]); mk_d = inp("ml_kT", [128, T + 3]); mv_d = inp("ml_v", [64, 128, 64])
        mif_d = inp("ml_if", [2, T]); mcw_d = inp("ml_cw", [128, 8])
        oml_d = P.dram("oml", [64, T], F32, "ExternalOutput")
        mv = P.sb([64, 128, 64], BF16, "mv")
        P.dma("pool", mv[:], mv_d[:, :, :], writes=["mv"])
        cw = P.sb([128, 8], F32, "cw")
        P.dma("sync", cw[:], mcw_d[:, :], writes=["cw"])
        cn32 = P.sb([128, 128], F32, "cn32"); cnbf = P.sb([128, 128], BF16, "cnbf")
        P.op("dve", lambda e: e.memset(cn32[:], 0.0), writes=["cn32"])
        P.op("dve", lambda e: e.memset(cnbf[:], 0.0), writes=["cnbf"])
        mcar = P.sb([128, 1], F32, "mcar")
        P.op("dve", lambda e: e.memset(mcar[:], 0.0), writes=["mcar"])
        xq = [P.sb([128, SBK + 3], F32, f"xq{i}") for i in range(2)]
        xk = [P.sb([128, SBK + 3], F32, f"xk{i}") for i in range(2)]
        gi = [P.sb([128, SBK], F32, f"gi{i}") for i in range(2)]
        gf = [P.sb([128, SBK], F32, f"gf{i}") for i in range(2)]
        q1T = [P.sb([128, SBK], BF16, f"q1T{i}") for i in range(2)]
        q2T = [P.sb([128, SBK], BF16, f"q2T{i}") for i in range(2)]
        k1T = [P.sb([128, SBK], BF16, f"k1T{i}") for i in range(2)]
        kwT = [P.sb([128, SBK], BF16, f"kwT{i}") for i in range(2)]
        kws = [P.sb([64, NCHK, 128], BF16, f"kws{i}") for i in range(2)]
        MSTs = [P.sb([64, NCHK, 64], BF16, f"MSTs{i}") for i in range(2)]
        cdec = [P.sb([128, NCHK], F32, f"cdec{i}") for i in range(2)]
        emt = [P.sb([128, SBK], F32, f"emt{i}") for i in range(2)]
        hsb = [P.sb([64, SBK], F32, f"hsb{i}") for i in range(2)]
        msm = P.sb([128, 4, NCHK], F32, "msm")
        Aq, Ak, Bb, Cc, Dd, Ee, Ff, Gg = S
        for sb in range(nsb):
            pz = sb % 2; t0 = sb * SBK
            kxq, kxk, kgi, kgf = f"xq{pz}", f"xk{pz}", f"gi{pz}", f"gf{pz}"
            P.dma("sync", xq[pz][:], mq_d[:, t0:t0 + SBK + 3], writes=[kxq])
            P.dma("sync", xk[pz][:], mk_d[:, t0:t0 + SBK + 3], writes=[kxk])
            P.dma("sync", gi[pz][:], mif_d[0:1, t0:t0 + SBK].broadcast_to([128, SBK]), writes=[kgi])
            P.dma("sync", gf[pz][:], mif_d[1:2, t0:t0 + SBK].broadcast_to([128, SBK]), writes=[kgf])
            for (x_, kx, dst, kd, w0) in ((xq[pz], kxq, Aq, "S0", 0), (xk[pz], kxk, Ak, "S1", 4)):
                ts("dve", dst[:], x_[:, 0:SBK], cw[:, w0:w0 + 1], None, ALU.mult, None, [kx, "cw"], [kd])
                for j in range(1, 4):
                    stt(dst[:], x_[:, j:j + SBK], cw[:, w0 + j:w0 + j + 1], dst[:], ALU.mult, ALU.add, [kx, "cw", kd], [kd])
                act(dst[:], dst[:], AF.Silu, [kd], [kd])
            act(Bb[:], gf[pz][:], AF.Exp, [kgf], ["S2"], scale=-1.0)
            act(Bb[:], Bb[:], AF.Ln, ["S2", "one1"], ["S2"], bias=one1[:, 0:1])
            P.op("dve", lambda e: e.tensor_tensor_scan(out=Cc[:], data0=rmask[:], data1=Bb[:], initial=0.0, op0=ALU.mult, op1=ALU.subtract),
                 ["rmask", "S2"], ["S3"])
            tt("dve", Dd[:], gi[pz][:], Cc[:], ALU.subtract, [kgi, "S3"], ["S4"])
            P.op("dve", lambda e: e.tensor_tensor_scan(out=Ee[:], data0=nmask[:], data1=Dd[:], initial=-1e30, op0=ALU.add, op1=ALU.max),
                 ["nmask", "S4"], ["S5"])
            tt("dve", msm[:, 0, :], Cc[:, 63::64], Ee[:, 63::64], ALU.add, ["S3", "S5"], ["msm"])
            P.op("dve", lambda e: e.tensor_tensor_scan(out=msm[:, 1, :], data0=Cc[:, 63::64], data1=msm[:, 0, :], initial=mcar[:, 0:1],
                                                        op0=ALU.add, op1=ALU.max), ["S3", "msm", "mcar"], ["msm"])
            P.op("dve", lambda e: e.tensor_copy(msm[:, 2, 1:NCHK], msm[:, 1, 0:NCHK - 1]), ["msm"], ["msm"])
            P.op("dve", lambda e: e.tensor_copy(msm[:, 2, 0:1], mcar[:, 0:1]), ["msm", "mcar"], ["msm"])
            P.op("dve", lambda e: e.tensor_copy(mcar[:, 0:1], msm[:, 1, NCHK - 1:NCHK]), ["msm"], ["mcar"])
            tt("dve", msm[:, 3, :], Cc[:, 63::64], msm[:, 1, :], ALU.subtract, ["S3", "msm"], ["msm"])
            tt("dve", msm[:, 0, :], msm[:, 3, :], msm[:, 2, :], ALU.add, ["msm"], ["msm"])
            act(cdec[pz][:], msm[:, 0, :], AF.Exp, ["msm"], [f"cdec{pz}"])
            tt("dve", c3(Ff[:]), c3(Cc[:]), msm[:, 2, :].unsqueeze(2).broadcast_to([128, NCHK, 64]), ALU.add, ["S3", "msm"], ["S6"])
            tt("dve", Ee[:], Cc[:], Ee[:], ALU.add, ["S3", "S5"], ["S5"])
            tt("dve", Ee[:], Ee[:], Ff[:], ALU.max, ["S5", "S6"], ["S5"])
            act(emt[pz][:], Ee[:], AF.Exp, ["S5"], [f"emt{pz}"], scale=-1.0)
            tt("dve", Ff[:], Ff[:], Ee[:], ALU.subtract, ["S6", "S5"], ["S6"])
            act(Ff[:], Ff[:], AF.Exp, ["S6"], ["S6"])
            stt(q2T[pz][:], Aq[:], HD_SCALE, Ff[:], ALU.mult, ALU.mult, ["S0", "S6"], [f"q2T{pz}"])
            tt("dve", Ee[:], Cc[:], Ee[:], ALU.subtract, ["S3", "S5"], ["S5"])
            act(Ee[:], Ee[:], AF.Exp, ["S5"], ["S5"])
            stt(q1T[pz][:], Aq[:], HD_SCALE, Ee[:], ALU.mult, ALU.mult, ["S0", "S5"], [f"q1T{pz}"])
            act(Gg[:], Dd[:], AF.Exp, ["S4"], ["S7"])
            tt("pool", k1T[pz][:], Ak[:], Gg[:], ALU.mult, ["S1", "S7"], [f"k1T{pz}"])
            tt("dve", c3(Dd[:]), c3(Dd[:]), msm[:, 3, :].unsqueeze(2).broadcast_to([128, NCHK, 64]), ALU.add, ["S4", "msm"], ["S4"])
            act(Dd[:], Dd[:], AF.Exp, ["S4"], ["S4"])
            tt("pool", kwT[pz][:], Ak[:], Dd[:], ALU.mult, ["S1", "S4"], [f"kwT{pz}"])
            for c in range(NCHK):
                mm(pb[0][0:64, c * 64:(c + 1) * 64], k1T[pz][:, c * 64:(c + 1) * 64], q1T[pz][:, c * 64:(c + 1) * 64], True, True,
                   [f"k1T{pz}", f"q1T{pz}"], ["pb0"])
            tt("dve", MSTs[pz][:], c3(pb[0][0:64, :]), tri[:].unsqueeze(1).broadcast_to([64, NCHK, 64]), ALU.mult, ["pb0", "tri"], [f"MSTs{pz}"])
            pb1b = pb[1][0:64, :].bitcast(BF16)
            for c in range(NCHK):
                tr(pb1b[:, c * 128:(c + 1) * 128], kwT[pz][:, c * 64:(c + 1) * 64], ident[:], [f"kwT{pz}", "ident"], ["pb1"])
            act(kws[pz][:], pb1b.rearrange("p (c d) -> p c d", d=128), AF.Copy, ["pb1"], [f"kws{pz}"])
            for c in range(NCHK):
                ch = sb * NCHK + c
                cs = slice(c * 64, (c + 1) * 64)
                mm(pb[2][0:64, cs], mv[:, ch, :], MSTs[pz][:, c, :], True, False, ["mv", f"MSTs{pz}"], ["pb2"])
                mm(pb[2][0:64, cs], cnbf[:, 0:64], q2T[pz][:, cs], False, True, ["cnbf", f"q2T{pz}"], ["pb2"])
                mm(pb[4][0:64, cs], onesb[0:64, 0:64], MSTs[pz][:, c, :], True, False, ["onesb", f"MSTs{pz}"], ["pb4"])
                mm(pb[4][0:64, cs], cnbf[:, 64:128], q2T[pz][:, cs], False, True, ["cnbf", f"q2T{pz}"], ["pb4"])
                mm(pb[3][:, 0:64], kws[pz][:, c, :], mv[:, ch, :], True, True, [f"kws{pz}", "mv"], ["pb3"])
                mm(pb[3][:, 64:128], kws[pz][:, c, :], onesb[0:64, 0:64], True, True, [f"kws{pz}", "onesb"], ["pb3"])
                stt(cn32[:], cn32[:], cdec[pz][:, c:c + 1], pb[3][:, 0:128], ALU.mult, ALU.add, ["cn32", f"cdec{pz}", "pb3"], ["cn32"])
                act(cnbf[:], cn32[:], AF.Copy, ["cn32"], ["cnbf"])
            stt(hsb[pz][:], pb[4][0:64, :], -1.0, emt[pz][0:64, :], ALU.mult, ALU.max, ["pb4", f"emt{pz}"], [f"hsb{pz}"])
            tt("dve", hsb[pz][:], pb[4][0:64, :], hsb[pz][:], ALU.max, ["pb4", f"hsb{pz}"], [f"hsb{pz}"])
            P.op("dve", lambda e, pz=pz: e.reciprocal(hsb[pz][:], hsb[pz][:]), [f"hsb{pz}"], [f"hsb{pz}"])
            tt("dve", hsb[pz][:], pb[2][0:64, :], hsb[pz][:], ALU.mult, ["pb2", f"hsb{pz}"], [f"hsb{pz}"])
            final.append(P.dma("sync", oml_d[:, t0:t0 + SBK], hsb[pz][:], reads=[f"hsb{pz}"]))
    if "ns" in parts:
        build_nsa(P, inp, final, pb, S, ident, ident32, onesb, ones32, epsb, act, tt, ts, stt, mm, tr, nqt)
    return P.finish(final)


def build_nsa(P, inp, final, pb, S, ident, ident32, onesb, ones32, epsb, act, tt, ts, stt, mm, tr, nqt):
    ksT_d = inp("ns_ksT", [128, T]); ksR_d = inp("ns_ksR", [128, T]); cosT_d = inp("cosT", [128, T]); sinT_d = inp("sinT", [128, T])
    kwT_d = inp("ns_kwT", [128, KWN]); kwR_d = inp("ns_kwR", [128, KWN]); cosW_d = inp("cosW", [128, KWN]); sinW_d = inp("sinW", [128, KWN])
    vs_d = inp("ns_vs", [T, 128]); vw_d = inp("ns_vw", [KWN, 128])
    kcT_d = inp("ns_kcT", [128, T + 32]); vcT_d = inp("ns_vcT", [128, T + 32])
    cw_d = inp("cmp_w", [2, 128, 32, 128]); cpe_d = inp("cmp_pe", [2, 128, 32])
    kcg_d = inp("kcg", [128, 128]); cosC_d = inp("cosC", [128, 4, 64]); sinC_d = inp("sinC", [128, 4, 64])
    qT_d = inp("ns_qT", [128, NQT, 512]); qR_d = inp("ns_qR", [128, NQT, 512]); cosQ_d = inp("cosQ", [128, NQT, 128]); sinQ_d = inp("sinQ", [128, NQT, 128])
    gate_d = inp("ns_gate", [3, NQT * 512]); nvec_d = inp("nvec", [128, 8])
    cmask_d = inp("cmask", [128, 4, 128]); cmask2_d = inp("cmask2", [128, 4, 128]); dmask_d = inp("dmask", [128, 4, 128]); wmask_d = inp("wmask", [128, 2, 5, 128])
    bonus_d = inp("bonus", [128, NQT, 128]); emat_d = inp("emat", [128, T]); ov_d = inp("ov", [128, 4, 128])
    yns_d = P.dram("yns", [128, NQT, 512], F32, "ExternalOutput")

    def ld(q, shape, dt, src, name):
        t_ = P.sb(shape, dt, name)
        P.dma(q, t_[:], src, writes=[name])
        return t_

    nvec = ld("sync", [128, 8], F32, nvec_d[:, :], "nvec")
    cmask = ld("pool", [128, 4, 128], BF16, cmask_d[:, :, :], "cmask")
    cmask2 = ld("pool", [128, 4, 128], BF16, cmask2_d[:, :, :], "cmask2")
    dmask = ld("pool", [128, 4, 128], BF16, dmask_d[:, :, :], "dmask")
    wmask = ld("pool", [128, 2, 5, 128], BF16, wmask_d[:, :, :, :], "wmask")
    bonus = ld("sync", [128, NQT, 128], F32, bonus_d[:, :, :], "bonus")
    emat = ld("pool", [128, T], BF16, emat_d[:, :], "emat")
    ov = ld("sync", [128, 4, 128], F32, ov_d[:, :, :], "ov")
    kcg = ld("sync", [128, 128], F32, kcg_d[:, :], "kcg")
    cosC = ld("sync", [128, 4, 64], F32, cosC_d[:, :, :], "cosC")
    sinC = ld("sync", [128, 4, 64], F32, sinC_d[:, :, :], "sinC")
    vs = P.sb([128, 64, 128], BF16, "vs")
    for h4 in range(4):
        P.dma("pool", vs[:, h4 * 16:(h4 + 1) * 16, :], vs_d[h4 * 2048:(h4 + 1) * 2048, :].rearrange("(n k) e -> k n e", k=128), writes=["vs"])
    vw = P.sb([128, 65, 128], BF16, "vw")
    for h5 in range(5):
        P.dma("pool", vw[:, h5 * 13:(h5 + 1) * 13, :], vw_d[h5 * 1664:(h5 + 1) * 1664, :].rearrange("(n k) e -> k n e", k=128), writes=["vw"])
    ksT = P.sb([128, T], BF16, "ksT")
    kwT = P.sb([128, KWN], BF16, "kwT")
    kcT = P.sb([128, 512], F32, "kcT")
    vc = P.sb([128, 4, 128], F32, "vc")

    aT = P.sb([128, T + 32], BF16, "aT")
    _a32 = aT[:].bitcast(F32)
    S_alt = [_a32[:, i * SBK:(i + 1) * SBK] for i in range(7)]
    SAK = [f"SA{i}" for i in range(7)]

    def rope_cm(x_src, r_src, c_src, s_src, n, gcol, dst, dkey, scale, nj=1, dst32=None, dkey32=None, alt=False):
        SS = S_alt if alt else S
        kp_ = "SA" if alt else "S"
        x, xr, cs, sn, t1, t2, rs = SS[0], SS[1], SS[2], SS[3], SS[4], SS[5], SS[6]
        nt = n // nj
        P.dma("sync", x[:, 0:n], x_src, writes=[kp_ + "0"]); P.dma("sync", xr[:, 0:n], r_src, writes=[kp_ + "1"])
        P.dma("sync", cs[:, 0:nt], c_src, writes=[kp_ + "2"]); P.dma("sync", sn[:, 0:nt], s_src, writes=[kp_ + "3"])
        act(t1[:, 0:n], x[:, 0:n], AF.Square, [kp_ + "0"], [kp_ + "4"])
        mm(pb[7][:, 0:n], ones32[:], t1[:, 0:n], True, True, [kp_ + "4", "ones32"], ["pb7"])
        act(rs[:, 0:n], pb[7][:, 0:n], AF.Sqrt, ["pb7", "epsb"], [kp_ + "6"], scale=1.0 / 128, bias=epsb[:, 0:1])
        P.op("dve", lambda e: e.reciprocal(rs[:, 0:n], rs[:, 0:n]), [kp_ + "6"], [kp_ + "6"])

        def v3(ap):
            return ap[:, 0:n].rearrange("p (j t) -> p j t", j=nj)

        def b3(ap):
            return ap[:, 0:nt].unsqueeze(1).broadcast_to([128, nj, nt])
        stt(v3(t1), v3(x), nvec[:, gcol:gcol + 1], b3(cs), ALU.mult, ALU.mult, [kp_ + "0", kp_ + "2", "nvec"], [kp_ + "4"])
        stt(v3(t2), v3(xr), nvec[:, gcol + 1:gcol + 2], b3(sn), ALU.mult, ALU.mult, [kp_ + "1", kp_ + "3", "nvec"], [kp_ + "5"])
        tt("pool", t1[:, 0:n], t1[:, 0:n], t2[:, 0:n], ALU.add, [kp_ + "4", kp_ + "5"], [kp_ + "4"])
        if dst32 is None:
            stt(dst, t1[:, 0:n], scale, rs[:, 0:n], ALU.mult, ALU.mult, [kp_ + "4", kp_ + "6"], [dkey])
        else:
            stt(dst32, t1[:, 0:n], scale, rs[:, 0:n], ALU.mult, ALU.mult, [kp_ + "4", kp_ + "6"], [dkey32])
            act(dst, dst32, AF.Copy, [dkey32], [dkey])

    for blk in range(T // 512):
        sl = slice(blk * 512, (blk + 1) * 512)
        rope_cm(ksT_d[:, sl], ksR_d[:, sl], cosT_d[:, sl], sinT_d[:, sl], 512, 2, ksT[:, sl], "ksT", 1.0, alt=(blk % 2 == 1))
    for blk in range(KWN // KWP):
        sl = slice(blk * KWP, (blk + 1) * KWP)
        rope_cm(kwT_d[:, sl], kwR_d[:, sl], cosW_d[:, sl], sinW_d[:, sl], KWP, 4, kwT[:, sl], "kwT", 1.0, alt=(blk % 2 == 1))

    cwb = P.sb([128, 32, 128], BF16, "cwb")
    pe32 = P.sb([128, 32], F32, "pe32")
    perep = P.sb([128, 32, 128], BF16, "perep")
    small = P.sb([128, 8], F32, "nsmall")
    for which in range(2):
        src = kcT_d if which == 0 else vcT_d
        P.dma("pool", aT[:, 0:4096], src[:, 0:4096], writes=["aT"] + SAK)
        P.dma("pool", aT[:, 4096:T + 32], src[:, 4096:T + 32], writes=["aT"] + SAK)
        P.dma("pool", cwb[:], cw_d[which], writes=["cwb"])
        P.dma("sync", pe32[:], cpe_d[which], writes=["pe32"])
        P.op("dve", lambda e: e.tensor_copy(perep[:], pe32[:].unsqueeze(2).broadcast_to([128, 32, 128])), ["pe32"], ["perep"])
        for ct in range(4):
            for l in range(32):
                base = 2048 * ct + l
                mm(pb[6][:, 0:128], aT[:, base:base + 2048:16], cwb[:, l, :], l == 0, False, ["aT", "cwb"] + SAK, ["pb6"])
            for l in range(32):
                mm(pb[6][:, 0:128], perep[:, l, :], cwb[:, l, :], False, l == 31, ["perep", "cwb"], ["pb6"])
            if which == 1:
                act(vc[:, ct, :], pb[6][:, 0:128], AF.Copy, ["pb6"], ["vc"])
                continue
            x = S[0][:, 0:128]; t1 = S[1][:, 0:128]; t2 = S[2][:, 0:128]; xn = S[3][:, 0:128]; kr = S[4][:, 0:128]
            act(x, pb[6][:, 0:128], AF.Copy, ["pb6"], ["S0"])
            P.op("act", lambda e, x=x, t1=t1: e.activation(out=t1, in_=x, func=AF.Square, accum_out=small[:, 0:1]), ["S0"], ["S1", "nsmall"])
            act(small[:, 1:2], small[:, 0:1], AF.Sqrt, ["nsmall", "epsb"], ["nsmall"], scale=1.0 / 128, bias=epsb[:, 0:1])
            P.op("dve", lambda e: e.reciprocal(small[:, 1:2], small[:, 1:2]), ["nsmall"], ["nsmall"])
            stt(xn, x, small[:, 1:2], kcg[:], ALU.mult, ALU.mult, ["S0", "nsmall", "kcg"], ["S3"])
            x1, x2 = xn[:, 0:64], xn[:, 64:128]
            cc, ss = cosC[:, ct, :], sinC[:, ct, :]
            tt("dve", t1[:, 0:64], x1, cc, ALU.mult, ["S3", "cosC"], ["S1"])
            tt("dve", t2[:, 0:64], x2, ss, ALU.mult, ["S3", "sinC"], ["S2"])
            tt("dve", kr[:, 0:64], t1[:, 0:64], t2[:, 0:64], ALU.subtract, ["S1", "S2"], ["S4"])
            tt("dve", t1[:, 64:128], x2, cc, ALU.mult, ["S3", "cosC"], ["S1"])
            tt("dve", t2[:, 64:128], x1, ss, ALU.mult, ["S3", "sinC"], ["S2"])
            tt("dve", kr[:, 64:128], t1[:, 64:128], t2[:, 64:128], ALU.add, ["S1", "S2"], ["S4"])
            tr(pb[6][:, 256:384], kr, ident32[:], ["S4", "ident32"], ["pb6"])
            act(kcT[:, ct * 128:(ct + 1) * 128], pb[6][:, 256:384], AF.Copy, ["pb6"], ["kcT"])

    qt = [P.sb([128, 512], BF16, f"nqt{i}") for i in range(2)]
    sg = [P.sb([128, 3, 512], F32, f"nsg{i}") for i in range(2)]
    pcs = [P.sb([128, 512], F32, f"npc{i}") for i in range(4)]
    qt32 = [P.sb([128, 512], F32, f"nqt32_{i}") for i in range(2)]
    pk = [P.sb([128, 512], BF16, f"npk{i}") for i in range(4)]
    rdc = P.sb([128, 512], F32, "rdc")
    wgt = P.sb([128, 512], F32, "wgt")
    acc = [P.sb([128, 512], F32, f"nacc{i}") for i in range(2)]
    score = P.sb([128, 128], F32, "score")
    sc2 = P.sb([128, 128], F32, "sc2")
    v8 = P.sb([128, 16], F32, "v8")
    sel = P.sb([128, 128], BF16, "sel")
    selT = P.sb([128, 128], BF16, "selT")
    for m in range(nqt):
        pz = m % 2
        q_ = qt[pz]; kq = f"nqt{pz}"
        rope_cm(qT_d[:, m, :], qR_d[:, m, :], cosQ_d[:, m, :], sinQ_d[:, m, :], 512, 0, q_[:], kq, HD_SCALE, nj=4, dst32=qt32[pz][:], dkey32=f"nqt32_{pz}", alt=(m % 2 == 1))
        ksg = f"nsg{pz}"
        for br in range(3):
            P.dma("sync", sg[pz][:, br, :], gate_d[br:br + 1, m * 512:(m + 1) * 512].broadcast_to([128, 512]), writes=[ksg])
        act(sg[pz][:], sg[pz][:], AF.Sigmoid, [ksg], [ksg])
        kacc = f"nacc{pz}"

        def finish_branch(br, po, pd, kpo, kpd, first):
            ts("dve", rdc[:], pd[:, :], 1e-30, None, ALU.max, None, [kpd], ["rdc"])
            P.op("dve", lambda e: e.reciprocal(rdc[:], rdc[:]), ["rdc"], ["rdc"])
            tt("pool", wgt[:], rdc[:], sg[pz][:, br, :], ALU.mult, ["rdc", ksg], ["wgt"])
            if first:
                tt("dve", acc[pz][:], po[:, :], wgt[:], ALU.mult, [kpo, "wgt"], [kacc])
            else:
                tt("dve", wgt[:], po[:, :], wgt[:], ALU.mult, [kpo, "wgt"], ["wgt"])
                tt("pool", acc[pz][:], acc[pz][:], wgt[:], ALU.add, [kacc, "wgt"], [kacc])

        nct = m // 4 + 1
        for ct in range(nct):
            sb_ = pb[ct % 2]; ks_ = f"pb{ct % 2}"
            mm(sb_[:, :], kcT[:, ct * 128:(ct + 1) * 128], qt32[pz][:], True, True, ["kcT", f"nqt32_{pz}"], [ks_])
            act(pcs[ct][:], sb_[:, :], AF.Exp, [ks_], [f"npc{ct}"])
            if ct == nct - 1:
                tt("dve", pcs[ct][:].rearrange("p (j t) -> p j t", j=4), pcs[ct][:].rearrange("p (j t) -> p j t", j=4),
                   cmask[:, m % 4, :].unsqueeze(1).broadcast_to([128, 4, 128]), ALU.mult, [f"npc{ct}", "cmask"], [f"npc{ct}"])
            if ct == nct - 2:
                tt("dve", pcs[ct][:].rearrange("p (j t) -> p j t", j=4), pcs[ct][:].rearrange("p (j t) -> p j t", j=4),
                   cmask2[:, m % 4, :].unsqueeze(1).broadcast_to([128, 4, 128]), ALU.mult, [f"npc{ct}", "cmask2"], [f"npc{ct}"])
            mm(pb[3][:, :], vc[:, ct, :], pcs[ct][:], ct == 0, ct == nct - 1, ["vc", f"npc{ct}"], ["pb3"])
            mm(pb[4][:, :], ones32[:], pcs[ct][:], ct == 0, ct == nct - 1, ["ones32", f"npc{ct}"], ["pb4"])
        finish_branch(0, pb[3], pb[4], "pb3", "pb4", True)
        for ct in range(nct):
            tt("dve", pcs[ct][:], pcs[ct][:], rdc[:], ALU.mult, [f"npc{ct}", "rdc"], [f"npc{ct}"])
        n_mm = 4 * nct; i_mm = 0
        for ct in range(nct):
            for j in range(4):
                mm(pb[7][:, 0:128], pcs[ct][:, j * 128:(j + 1) * 128], ov[:, ct, :], i_mm == 0, i_mm == n_mm - 1, [f"npc{ct}", "ov"], ["pb7"])
                i_mm += 1
        tt("dve", score[:], pb[7][:, 0:128], bonus[:, m, :], ALU.add, ["pb7", "bonus"], ["score"])
        P.op("dve", lambda e: e.max(out=v8[:, 0:8], in_=score[:]), ["score"], ["v8"])
        P.op("dve", lambda e: e.match_replace(out=sc2[:], in_to_replace=v8[:, 0:8], in_values=score[:], imm_value=-3e38), ["score", "v8"], ["sc2"])
        P.op("dve", lambda e: e.max(out=v8[:, 8:16], in_=sc2[:]), ["sc2"], ["v8"])
        ts("dve", v8[:, 15:16], v8[:, 15:16], -5e29, None, ALU.max, None, ["v8"], ["v8"])
        ts("dve", sel[:], score[:], v8[:, 15:16], None, ALU.is_ge, None, ["score", "v8"], ["sel"])
        p7b = pb[7][:, 256:512].bitcast(BF16)
        tr(p7b[:, 0:128], sel[:], ident[:], ["sel", "ident"], ["pb7"])
        act(selT[:], p7b[:, 0:128], AF.Copy, ["pb7"], ["selT"])
        nkt = 4 * m + 4
        LAG = 2

        def sel_a(kt):
            sb_ = pb[kt % 2]; ks_ = f"pb{kt % 2}"
            p_ = pk[kt % 4]; kp = f"npk{kt % 4}"
            mb_ = 2 if kt % 2 == 0 else 7
            msl = pb[mb_][:, 0:128]; mkey = f"pb{mb_}"
            mm(sb_[:, :], ksT[:, kt * 128:(kt + 1) * 128], q_[:], True, True, ["ksT", kq], [ks_])
            mm(msl, emat[:, kt * 128:(kt + 1) * 128], selT[:], True, True, ["emat", "selT"], [mkey])
            act(p_[:], sb_[:, :], AF.Exp, [ks_], [kp])
            p3 = p_[:].rearrange("p (j t) -> p j t", j=4)
            tt("dve", p3, p3, msl.unsqueeze(1).broadcast_to([128, 4, 128]), ALU.mult, [kp, mkey], [kp])
            if kt >= 4 * m:
                tt("dve", p3, p3, dmask[:, kt - 4 * m, :].unsqueeze(1).broadcast_to([128, 4, 128]), ALU.mult, [kp, "dmask"], [kp])

        def sel_b(kt):
            p_ = pk[kt % 4]; kp = f"npk{kt % 4}"
            mm(pb[5][:, :], vs[:, kt, :], p_[:], kt == 0, kt == nkt - 1, ["vs", kp], ["pb5"])
            mm(pb[6][:, :], onesb[:], p_[:], kt == 0, kt == nkt - 1, ["onesb", kp], ["pb6"])
        for kt in range(nkt + LAG):
            if kt < nkt:
                sel_a(kt)
            if kt >= LAG:
                sel_b(kt - LAG)
        finish_branch(1, pb[5], pb[6], "pb5", "pb6", False)
        for w in range(5):
            n_ = 4 * m + w
            sb_ = pb[w % 2]; ks_ = f"pb{w % 2}"
            p_ = pk[w % 3]; kp = f"npk{w % 3}"
            mm(sb_[:, :], kwT[:, n_ * 128:(n_ + 1) * 128], q_[:], True, True, ["kwT", kq], [ks_])
            act(p_[:], sb_[:, :], AF.Exp, [ks_], [kp])
            p3 = p_[:].rearrange("p (j t) -> p j t", j=4)
            tt("dve", p3, p3, wmask[:, 0 if m == 0 else 1, w, :].unsqueeze(1).broadcast_to([128, 4, 128]), ALU.mult, [kp, "wmask"], [kp])
            mm(pb[3][:, :], vw[:, n_, :], p_[:], w == 0, w == 4, ["vw", kp], ["pb3"])
            mm(pb[4][:, :], onesb[:], p_[:], w == 0, w == 4, ["onesb", kp], ["pb4"])
        finish_branch(2, pb[3], pb[4], "pb3", "pb4", False)
        final.append(P.dma("sync", yns_d[:, m, :], acc[pz][:], reads=[kacc]))


def _roll64(a):
    return np.roll(a, 64, axis=0)


_CONST_CACHE = {}


def _rope_tables(pos):
    inv = 10000.0 ** (-np.arange(64, dtype=np.float64) / 64.0)
    inv = inv.astype(np.float32).astype(np.float64)
    ang = (pos.astype(np.float32)[None, :] * inv.astype(np.float32)[:, None]).astype(np.float64)
    c = np.cos(ang); s = np.sin(ang)
    cos = np.concatenate([c, c], axis=0).astype(np.float32)
    sin = np.concatenate([-s, s], axis=0).astype(np.float32)
    return np.ascontiguousarray(cos), np.ascontiguousarray(sin)


def prep_rec(zT, layer, inputs):
    ident = np.eye(128, dtype=np.float32)
    tri = (np.arange(64)[:, None] <= np.arange(64)[None, :]).astype(np.float32)
    lmask = np.zeros((128, DEPTH), np.float32); lmask[:, 1:layer + 1] = 1.0
    maps = []
    for c in range(NCORE):
        hh, half = c // 2, c % 2

        def vtok(row0):
            v = zT[row0 + hh * 128 + half * 64: row0 + hh * 128 + half * 64 + 64, :].T
            return np.ascontiguousarray(v.reshape(128, 64, 64).transpose(1, 0, 2))

        def pad3(a):
            return np.ascontiguousarray(np.concatenate([np.zeros((128, 3), np.float32), a], axis=1))
        conv = inputs["mlstm_conv"][layer]
        cw = np.concatenate([conv[:, hh * 128:(hh + 1) * 128].T, conv[:, 512 + hh * 128:512 + (hh + 1) * 128].T], axis=1)
        maps.append({
            "ident": ident, "tri64": tri,
            "hg_qT": np.ascontiguousarray(zT[O_HGQ + hh * 128:O_HGQ + (hh + 1) * 128]),
            "hg_fT": np.ascontiguousarray(zT[O_HGF + hh * 128:O_HGF + (hh + 1) * 128]),
            "hg_v": vtok(O_HGI),
            "hg_logit": np.ascontiguousarray(inputs["hgrn_lb_logits"][:, hh * 128:(hh + 1) * 128].T.astype(np.float32)),
            "hg_lmask": lmask,
            "ml_qT": pad3(zT[Q_MLQ + hh * 128:Q_MLQ + (hh + 1) * 128]),
            "ml_kT": pad3(zT[Q_MLK + hh * 128:Q_MLK + (hh + 1) * 128]),
            "ml_v": vtok(Q_MLV),
            "ml_if": np.ascontiguousarray(np.stack([zT[Q_MLI + hh], zT[Q_MLF + hh]], axis=0)),
            "ml_cw": np.ascontiguousarray(cw.astype(np.float32)),
        })
    return maps


def nsa_consts(r):
    key = ("nsa", r)
    if key in _CONST_CACHE:
        return _CONST_CACHE[key]
    cd = {}
    cd["cosT"], cd["sinT"] = _rope_tables(np.arange(T))
    posw = np.arange(KWN) + (r - 4) * 128
    cd["cosW"], cd["sinW"] = _rope_tables(np.clip(posw, 0, T - 1))
    cpos = 16 * np.arange(512) + 31
    inv = (10000.0 ** (-np.arange(64, dtype=np.float64) / 64.0)).astype(np.float32)
    angc = (cpos.astype(np.float32)[:, None] * inv[None, :]).astype(np.float64)
    cd["cosC"] = np.ascontiguousarray(np.cos(angc).astype(np.float32).reshape(4, 128, 64).transpose(1, 0, 2))
    cd["sinC"] = np.ascontiguousarray(np.sin(angc).astype(np.float32).reshape(4, 128, 64).transpose(1, 0, 2))
    qpos = ((4 * np.arange(NQT)[:, None] + r) * 128 + np.arange(128)[None, :]).reshape(-1)
    cq, sq = _rope_tables(qpos)
    cd["cosQ"] = np.ascontiguousarray(cq.reshape(128, NQT, 128)); cd["sinQ"] = np.ascontiguousarray(sq.reshape(128, NQT, 128))
    cl = np.arange(128)[:, None, None]; b4 = np.arange(4)[None, :, None]; tl = np.arange(128)[None, None, :]
    cd["cmask"] = (16 * cl + 31 <= 512 * b4 + 128 * r + tl).astype(np.float32)
    cd["cmask2"] = (16 * cl + 31 <= 2048 + 512 * b4 + 128 * r + tl).astype(np.float32)
    dm = np.zeros((128, 4, 128), np.float32)
    for jj in range(4):
        if jj < r:
            dm[:, jj, :] = 1.0
        elif jj == r:
            dm[:, jj, :] = (np.arange(128)[:, None] <= np.arange(128)[None, :])
    cd["dmask"] = dm
    wm = np.ones((128, 2, 5, 128), np.float32)
    kl = np.arange(128)[:, None]; tl2 = np.arange(128)[None, :]
    wm[:, :, 0, :] = (kl > tl2)[:, None, :]
    wm[:, :, 4, :] = (kl <= tl2)[:, None, :]
    for w in range(5):
        if w + r - 4 < 0:
            wm[:, 0, w, :] = 0.0
    cd["wmask"] = wm
    t_abs = (4 * np.arange(NQT)[None, :, None] + r) * 128 + np.arange(128)[:, None, None]
    cur = t_abs // 64
    s_ = np.arange(128)[None, None, :]
    elig = s_ <= cur
    forced = (s_ == 0) | (s_ == cur) | (s_ == cur - 1)
    cd["bonus"] = np.where(elig, np.where(forced, 1e4, 0.0), -1e30).astype(np.float32)
    cd["emat"] = (np.arange(T)[None, :] // 64 == np.arange(128)[:, None]).astype(np.float32)
    c_start = 16 * np.arange(512)
    s_start = 64 * np.arange(128)
    ovm = ((c_start[:, None] < s_start[None, :] + 64) & (c_start[:, None] + 32 > s_start[None, :])).astype(np.float32)
    cd["ov"] = np.ascontiguousarray(ovm.reshape(4, 128, 128).transpose(1, 0, 2))
    cd["ident"] = np.eye(128, dtype=np.float32)
    cd["tri64"] = (np.arange(64)[:, None] <= np.arange(64)[None, :]).astype(np.float32)
    cd = {k: np.ascontiguousarray(v) for k, v in cd.items()}
    _CONST_CACHE[key] = cd
    return cd


def prep_nsa(zT, layer, inputs):
    maps = []
    kg = inputs["nsa_k_norm"][layer].astype(np.float32)
    qg = inputs["nsa_q_norm"][layer].astype(np.float32)
    nvec = np.zeros((128, 8), np.float32)
    nvec[:, 0] = qg; nvec[:, 1] = np.roll(qg, 64); nvec[:, 2] = kg[1]; nvec[:, 3] = np.roll(kg[1], 64); nvec[:, 4] = kg[2]; nvec[:, 5] = np.roll(kg[2], 64)
    cmp_w = np.ascontiguousarray(inputs["nsa_cmp_w"][layer].transpose(0, 2, 1, 3).astype(np.float32))
    cmp_pe = np.ascontiguousarray(inputs["nsa_cmp_pos"][layer].transpose(0, 2, 1).astype(np.float32))
    kcg = np.ascontiguousarray(np.broadcast_to(kg[0][None, :], (128, 128)).astype(np.float32))
    for c in range(NCORE):
        g, r = c // 4, c % 4
        mp = dict(nsa_consts(r))
        ks = zT[O_KS + g * 128:O_KS + (g + 1) * 128]
        kw = zT[O_KW + g * 128:O_KW + (g + 1) * 128]
        vs = zT[O_VS + g * 128:O_VS + (g + 1) * 128]
        vw = zT[O_VW + g * 128:O_VW + (g + 1) * 128]
        mp["ns_ksT"] = np.ascontiguousarray(ks); mp["ns_ksR"] = np.ascontiguousarray(_roll64(ks))

        def shift(a):
            out = np.zeros((128, KWN), np.float32)
            lo = (r - 4) * 128
            s0 = max(0, lo); e0 = min(T, lo + KWN)
            out[:, s0 - lo:e0 - lo] = a[:, s0:e0]
            return out
        kws = shift(kw)
        mp["ns_kwT"] = kws; mp["ns_kwR"] = np.ascontiguousarray(_roll64(kws))
        mp["ns_vs"] = np.ascontiguousarray(vs.T); mp["ns_vw"] = np.ascontiguousarray(shift(vw).T)

        def pad32(a):
            return np.ascontiguousarray(np.concatenate([a, np.zeros((128, 32), np.float32)], axis=1))
        mp["ns_kcT"] = pad32(zT[O_KC + g * 128:O_KC + (g + 1) * 128]); mp["ns_vcT"] = pad32(zT[O_VC + g * 128:O_VC + (g + 1) * 128])
        mp["cmp_w"] = cmp_w; mp["cmp_pe"] = cmp_pe; mp["kcg"] = kcg; mp["nvec"] = nvec
        q = zT[O_NSQ + g * 512:O_NSQ + (g + 1) * 512].reshape(4, 128, 16, 4, 128)[:, :, :, r, :]
        qT = np.ascontiguousarray(q.transpose(1, 2, 0, 3).reshape(128, NQT, 512))
        mp["ns_qT"] = qT
        mp["ns_qR"] = np.ascontiguousarray(_roll64(qT))
        gt = zT[Q_GATE + g * 12:Q_GATE + (g + 1) * 12].reshape(4, 3, 16, 4, 128)[:, :, :, r, :]
        mp["ns_gate"] = np.ascontiguousarray(gt.transpose(1, 2, 0, 3).reshape(3, NQT * 512))
        maps.append(mp)
    return maps


_PROGS = {}


def _prog(key, builder):
    if key not in _PROGS:
        _PROGS[key] = builder()
    return _PROGS[key]


def _vec(v, C):
    return np.asarray(v, np.float32).reshape(C, 128).T


def _run(nc, maps):
    return run_bass_kernel_spmd(nc, maps, core_ids=list(range(NCORE))).results


def kernel(**inputs):
    inputs = {k: np.asarray(v) for k, v in inputs.items()}
    wb = run_cast(inputs)
    xT = np.ascontiguousarray(inputs["x"][0].T.astype(np.float32))
    memT = np.ascontiguousarray(inputs["mem"][0].T.astype(np.float32))

    def vpre(layer):
        bperm = np.zeros(53 * 128, np.float32)
        bperm[:D_IN] = inputs["b_in"][layer][PERM]
        return np.ascontiguousarray(np.concatenate([_vec(inputs["norm_mix"][layer], 16), bperm.reshape(53, 128).T], axis=1))

    def vpost(layer):
        return np.ascontiguousarray(np.concatenate([
            _vec(inputs["norm_xattn"][layer], 16), _vec(inputs["norm_mem"][layer], 16), _vec(inputs["norm_mlp"][layer], 16),
            _vec(inputs["hgrn_norm"][layer], 4), _vec(inputs["mlstm_norm"][layer], 4),
            _vec(inputs["xa_q_norm"][layer], 4), _vec(inputs["xa_k_norm"][layer], 4)], axis=1))

    def cols(a, c):
        return np.ascontiguousarray(a[:, c * TPC:(c + 1) * TPC])

    nc = _prog("pre", lambda: build_token_phase(False, True))
    res = _run(nc, [{"xT": cols(xT, c), "w_in": wb["w_in"][0], "vpre": vpre(0)} for c in range(NCORE)])
    zT = np.concatenate([np.asarray(r["zT_out"]) for r in res], axis=1)

    for layer in range(DEPTH):
        nc = _prog("rec", lambda: build_mixer_phase(("hg", "ml")))
        res = _run(nc, prep_rec(zT, layer, inputs))
        ohg = np.zeros((512, T), np.float32); oml = np.zeros((512, T), np.float32)
        for c in range(NCORE):
            hh, half = c // 2, c % 2
            r0 = hh * 128 + half * 64
            ohg[r0:r0 + 64] = np.asarray(res[c]["ohg"]); oml[r0:r0 + 64] = np.asarray(res[c]["oml"])
        nc = _prog("nsa", lambda: build_mixer_phase(("ns",)))
        res = _run(nc, prep_nsa(zT, layer, inputs))
        yns = np.zeros((1024, T), np.float32)
        for c in range(NCORE):
            g, r = c // 4, c % 4
            o = np.asarray(res[c]["yns"]).reshape(128, NQT, 4, 128)
            dst = yns[g * 512:(g + 1) * 512].reshape(4, 128, NQT, 4, 128)
            dst[:, :, :, r, :] = o.transpose(2, 0, 1, 3)
        gT = zT[O_HGG:O_HGG + 512]; opT = zT[Q_MLO:Q_MLO + 512]
        last = layer == DEPTH - 1
        nc = _prog("post" if last else "postpre", lambda: build_token_phase(True, not last))
        maps = []
        for c in range(NCORE):
            mp = {"xT": cols(xT, c), "ohgT": cols(ohg, c), "omlT": cols(oml, c), "ynsT": cols(yns, c), "gT": cols(gT, c), "opT": cols(opT, c),
                  "memT": memT, "w_out": wb["w_out"][layer], "wq": wb["xa_wq"][layer], "wk": wb["xa_wk"][layer], "wv": wb["xa_wv"][layer],
                  "wo": wb["xa_wo"][layer], "w1": wb["mlp_w1"][layer], "w2": wb["mlp_w2"][layer], "vpost": vpost(layer)}
            if not last:
                mp["w_in"] = wb["w_in"][layer + 1]; mp["vpre"] = vpre(layer + 1)
            maps.append(mp)
        res = _run(nc, maps)
        xT = np.concatenate([np.asarray(r["xT_out"]) for r in res], axis=1)
        if not last:
            zT = np.concatenate([np.asarray(r["zT_out"]) for r in res], axis=1)
    return np.ascontiguousarray(xT.T)[None].astype(np.float32)
```
